# Optimizing a Trainium2 kernel written in Bass

```python
import math
import jax, jax.numpy as jnp
from jax import lax
import numpy as np

D_MODEL = 1024
BATCH = 8
SEQ = 8192
DEPTH = 1
DEC_BATCH = 4
DEC_SEQ = 8192
PAST_LEN = 128

HEAD_DIM = 64
N_HEADS_A = D_MODEL // (2 * HEAD_DIM)
N_HEADS_B = D_MODEL // (2 * HEAD_DIM)
N_KV_B = N_HEADS_B // 4
WIDTH_A = N_HEADS_A * HEAD_DIM
WIDTH_B = N_HEADS_B * HEAD_DIM
MIX_WIDTH = WIDTH_A + WIDTH_B
KV_WIDTH_B = N_KV_B * HEAD_DIM
IN_COLS = 3 * WIDTH_A + WIDTH_B + 2 * KV_WIDTH_B
IN_SPLITS = (WIDTH_A, 2 * WIDTH_A, 3 * WIDTH_A, 3 * WIDTH_A + WIDTH_B, 3 * WIDTH_A + WIDTH_B + KV_WIDTH_B)
DILATED_PATTERNS = ((128, 1), (512, 4), (2048, 16))
N_MEM = 256
N_HEADS_MEM = 4
HEAD_DIM_MEM = D_MODEL // N_HEADS_MEM
D_FF = 2816
CONV_W = 3
GRID_W = 64
ROPE_THETA = 10000.0
Q_BLOCK = 128
LN_EPS = 1e-5
RMS_EPS = 1e-6

kernel_name = 'hybrid_dilated_axial_gqa_encoder'


def layer_norm(x, g, b):
    xf = x.astype(jnp.float32)
    mu = jnp.mean(xf, axis=-1, keepdims=True)
    var = jnp.mean(jnp.square(xf - mu), axis=-1, keepdims=True)
    return ((xf - mu) * lax.rsqrt(var + LN_EPS) * g + b).astype(x.dtype)


def rms_norm(x, g):
    xf = x.astype(jnp.float32)
    ms = jnp.mean(jnp.square(xf), axis=-1, keepdims=True)
    return (xf * lax.rsqrt(ms + RMS_EPS) * g).astype(x.dtype)


def rope_angles(pos, dim):
    inv = ROPE_THETA ** (-jnp.arange(0, dim, 2, dtype=jnp.float32) / dim)
    ang = pos.astype(jnp.float32)[:, None] * inv[None, :]
    return jnp.cos(ang), jnp.sin(ang)


def apply_rope(x, cos, sin):
    x1, x2 = jnp.split(x, 2, axis=-1)
    c = cos[:, None, :].astype(x.dtype)
    s = sin[:, None, :].astype(x.dtype)
    return jnp.concatenate([x1 * c - x2 * s, x2 * c + x1 * s], axis=-1)


def apply_axial_rope(x):
    n = x.shape[1]
    rows = n // GRID_W
    row = jnp.repeat(jnp.arange(rows), GRID_W)
    col = jnp.tile(jnp.arange(GRID_W), rows)
    half = x.shape[-1] // 2
    cr, sr = rope_angles(row, half)
    cc, sc = rope_angles(col, half)
    xr, xc = jnp.split(x, 2, axis=-1)
    return jnp.concatenate([apply_rope(xr, cr, sr), apply_rope(xc, cc, sc)], axis=-1)


def dilated_window_attention(q, k, v, window, dilation):
    b, n, h, e = q.shape
    half = window // (2 * dilation)
    blk = half
    L = n // dilation
    nb = -(-L // blk)
    lp = nb * blk

    def phases(t):
        return t.reshape(b, L, dilation, h, e).transpose(0, 2, 1, 3, 4)

    qp = jnp.pad(phases(q), ((0, 0), (0, 0), (0, lp - L), (0, 0), (0, 0))).reshape(b, dilation, nb, blk, h, e)

    def key_windows(t):
        tp = jnp.pad(phases(t), ((0, 0), (0, 0), (blk, lp - L + blk), (0, 0), (0, 0)))
        tp = tp.reshape(b, dilation, nb + 2, blk, h, e)
        return jnp.concatenate([tp[:, :, :nb], tp[:, :, 1:nb + 1], tp[:, :, 2:]], axis=3)

    kw = key_windows(k)
    vw = key_windows(v)
    qi = jnp.arange(nb)[:, None, None] * blk + jnp.arange(blk)[None, :, None]
    kj = (jnp.arange(nb)[:, None, None] - 1) * blk + jnp.arange(3 * blk)[None, None, :]
    valid = ((jnp.abs(kj - qi) <= half) & (kj >= 0) & (kj < L)) | (kj == qi)

    s = jnp.einsum('bdnqhe,bdnkhe->bdnhqk', qp, kw).astype(jnp.float32) * (e ** -0.5)
    s = jnp.where(valid[None, None, :, None], s, -jnp.inf)
    m = jnp.max(s, axis=-1, keepdims=True)
    p = jnp.exp(s - m)
    den = jnp.sum(p, axis=-1, keepdims=True)
    o = jnp.einsum('bdnhqk,bdnkhe->bdnqhe', (p / den).astype(v.dtype), vw)
    lse = (m + jnp.log(den))[..., 0]
    o = o.reshape(b, dilation, lp, h, e)[:, :, :L].transpose(0, 2, 1, 3, 4).reshape(b, n, h, e)
    lse = lse.transpose(0, 1, 2, 4, 3).reshape(b, dilation, lp, h)[:, :, :L]
    lse = lse.transpose(0, 2, 1, 3).reshape(b, n, h)
    return o, lse


def gqa_blocked(q, k, v):
    b, n, hq, e = q.shape
    hk = k.shape[2]
    g = hq // hk
    nblk = n // Q_BLOCK
    qb = q.reshape(b, nblk, Q_BLOCK, hk, g, e).transpose(1, 0, 2, 3, 4, 5)

    def one_block(qblk):
        s = jnp.einsum('bqkge,bske->bkgqs', qblk, k).astype(jnp.float32) * (e ** -0.5)
        p = jax.nn.softmax(s, axis=-1)
        return jnp.einsum('bkgqs,bske->bqkge', p.astype(v.dtype), v)

    o = lax.map(one_block, qb)
    return o.transpose(1, 0, 2, 3, 4, 5).reshape(b, n, hq, e)


def memory_cross_attention(x, mem, wq, wk, wv, wo):
    b, n, _ = x.shape
    m = mem.shape[1]
    q = (x @ wq).reshape(b, n, N_HEADS_MEM, HEAD_DIM_MEM)
    k = (mem @ wk).reshape(b, m, N_HEADS_MEM, HEAD_DIM_MEM)
    v = (mem @ wv).reshape(b, m, N_HEADS_MEM, HEAD_DIM_MEM)
    s = jnp.einsum('bnhe,bmhe->bhnm', q, k).astype(jnp.float32) * (HEAD_DIM_MEM ** -0.5)
    p = jax.nn.softmax(s, axis=-1)
    o = jnp.einsum('bhnm,bmhe->bnhe', p.astype(v.dtype), v).reshape(b, n, D_MODEL)
    return o @ wo


def conv_glu_ffn(x, w_up, conv_w, conv_b, w_down):
    u = x @ w_up
    gate, val = jnp.split(u, 2, axis=-1)
    gate = lax.conv_general_dilated(
        gate, conv_w[:, None, :], window_strides=(1,),
        padding=((CONV_W // 2, CONV_W // 2),),
        dimension_numbers=('NWC', 'WIO', 'NWC'),
        feature_group_count=D_FF) + conv_b
    h = jax.nn.gelu(gate, approximate=False) * val
    return h @ w_down


def encoder(x, mem, w_in, q_norm_g, k_norm_g, out_norm_a, out_norm_b, w_o, ln1_g, ln1_b,
            wc_q, wc_k, wc_v, wc_o, ln2_g, ln2_b, w_up, conv_w, conv_b, w_down, ln3_g, ln3_b):
    b, n, _ = x.shape
    alpha = (2 * DEPTH) ** 0.25
    cos, sin = rope_angles(jnp.arange(n), HEAD_DIM)
    for l in range(DEPTH):
        proj = x @ w_in[l]
        qa, ka, va, qb, kb, vb = jnp.split(proj, IN_SPLITS, axis=-1)
        qa = apply_rope(qa.reshape(b, n, N_HEADS_A, HEAD_DIM), cos, sin)
        ka = apply_rope(ka.reshape(b, n, N_HEADS_A, HEAD_DIM), cos, sin)
        va = va.reshape(b, n, N_HEADS_A, HEAD_DIM)
        outs, lses = [], []
        for window, dilation in DILATED_PATTERNS:
            o_i, lse_i = dilated_window_attention(qa, ka, va, window, dilation)
            outs.append(o_i)
            lses.append(lse_i)
        wts = jax.nn.softmax(jnp.stack(lses), axis=0)
        oa = jnp.einsum('pbnh,pbnhe->bnhe', wts, jnp.stack(outs).astype(jnp.float32)).astype(x.dtype)
        oa = rms_norm(oa.reshape(b, n, WIDTH_A), out_norm_a[l])

        qb = apply_axial_rope(rms_norm(qb.reshape(b, n, N_HEADS_B, HEAD_DIM), q_norm_g[l]))
        kb = apply_axial_rope(rms_norm(kb.reshape(b, n, N_KV_B, HEAD_DIM), k_norm_g[l]))
        vb = vb.reshape(b, n, N_KV_B, HEAD_DIM)
        ob = rms_norm(gqa_blocked(qb, kb, vb).reshape(b, n, WIDTH_B), out_norm_b[l])

        mix = jnp.concatenate([oa, ob], axis=-1) @ w_o[l]
        x = layer_norm(alpha * x + mix, ln1_g[l], ln1_b[l])
        x = layer_norm(alpha * x + memory_cross_attention(x, mem, wc_q[l], wc_k[l], wc_v[l], wc_o[l]), ln2_g[l], ln2_b[l])
        x = layer_norm(alpha * x + conv_glu_ffn(x, w_up[l], conv_w[l], conv_b[l], w_down[l]), ln3_g[l], ln3_b[l])
    return x


def setup_inputs(seed: int = 0) -> dict:
    key = jax.random.key(seed)
    ks = jax.random.split(key, 28)
    f32 = jnp.float32
    beta = (8 * DEPTH) ** -0.25

    def normal(k, shape, scale):
        return jax.random.normal(k, shape, f32) * scale

    def gain(k, shape):
        return 1.0 + 0.02 * jax.random.normal(k, shape, f32)

    return {
        'x_prompt': normal(ks[0], (BATCH, SEQ, D_MODEL), 1.0),
        'x_sample': normal(ks[1], (DEC_BATCH, DEC_SEQ, D_MODEL), 1.0),
        'mem_prompt': normal(ks[2], (BATCH, N_MEM, D_MODEL), 1.0),
        'mem_sample': normal(ks[3], (DEC_BATCH, N_MEM, D_MODEL), 1.0),
        'w_in': normal(ks[4], (DEPTH, D_MODEL, IN_COLS), D_MODEL ** -0.5),
        'q_norm_g': gain(ks[5], (DEPTH, HEAD_DIM)),
        'k_norm_g': gain(ks[6], (DEPTH, HEAD_DIM)),
        'out_norm_a': gain(ks[7], (DEPTH, WIDTH_A)),
        'out_norm_b': gain(ks[8], (DEPTH, WIDTH_B)),
        'w_o': normal(ks[9], (DEPTH, MIX_WIDTH, D_MODEL), beta * MIX_WIDTH ** -0.5),
        'ln1_g': gain(ks[10], (DEPTH, D_MODEL)),
        'ln1_b': normal(ks[11], (DEPTH, D_MODEL), 0.02),
        'wc_q': normal(ks[12], (DEPTH, D_MODEL, D_MODEL), D_MODEL ** -0.5),
        'wc_k': normal(ks[13], (DEPTH, D_MODEL, D_MODEL), D_MODEL ** -0.5),
        'wc_v': normal(ks[14], (DEPTH, D_MODEL, D_MODEL), D_MODEL ** -0.5),
        'wc_o': normal(ks[15], (DEPTH, D_MODEL, D_MODEL), beta * D_MODEL ** -0.5),
        'ln2_g': gain(ks[16], (DEPTH, D_MODEL)),
        'ln2_b': normal(ks[17], (DEPTH, D_MODEL), 0.02),
        'w_up': normal(ks[18], (DEPTH, D_MODEL, 2 * D_FF), D_MODEL ** -0.5),
        'conv_w': normal(ks[19], (DEPTH, CONV_W, D_FF), CONV_W ** -0.5),
        'conv_b': normal(ks[20], (DEPTH, D_FF), 0.02),
        'w_down': normal(ks[21], (DEPTH, D_FF, D_MODEL), beta * D_FF ** -0.5),
        'ln3_g': gain(ks[22], (DEPTH, D_MODEL)),
        'ln3_b': normal(ks[23], (DEPTH, D_MODEL), 0.02),
    }


def reference(x_prompt, x_sample, mem_prompt, mem_sample, w_in, q_norm_g, k_norm_g, out_norm_a, out_norm_b,
              w_o, ln1_g, ln1_b, wc_q, wc_k, wc_v, wc_o, ln2_g, ln2_b, w_up, conv_w, conv_b, w_down, ln3_g, ln3_b):
    y_prompt = encoder(x_prompt, mem_prompt, w_in, q_norm_g, k_norm_g, out_norm_a, out_norm_b, w_o, ln1_g, ln1_b,
                       wc_q, wc_k, wc_v, wc_o, ln2_g, ln2_b, w_up, conv_w, conv_b, w_down, ln3_g, ln3_b)
    y_sample = encoder(x_sample, mem_sample, w_in, q_norm_g, k_norm_g, out_norm_a, out_norm_b, w_o, ln1_g, ln1_b,
                       wc_q, wc_k, wc_v, wc_o, ln2_g, ln2_b, w_up, conv_w, conv_b, w_down, ln3_g, ln3_b)
    return (y_prompt, y_sample)
```

```python
from contextlib import ExitStack

import numpy as np
import concourse.bass as bass
import concourse.mybir as mybir
from concourse.bass_utils import run_bass_kernel_spmd

F32 = mybir.dt.float32
BF16 = mybir.dt.bfloat16
AF = mybir.ActivationFunctionType
ALU = mybir.AluOpType

D = 1024
KC = 8
DFF = 2816
FC = 22
NMEM = 256
NBLK = 33
ALPHA = float(2.0 ** 0.25)
LN_EPS = 1e-5
RMS_EPS = 1e-6
THETA = 10000.0
QA, QAS, KA, KAS, VA, QB, QBS, KB0, KB1, KB0S, KB1S, VB = 0, 4, 8, 12, 16, 20, 24, 28, 29, 30, 31, 32
NMASK = 20


class Trk:
    def __init__(self, nc, es):
        self.nc = nc
        self.E = {'pe': nc.tensor, 'act': nc.scalar, 'dve': nc.vector, 'pool': nc.gpsimd, 'sp': nc.sync}
        self.sem = {k: es.enter_context(nc.semaphore('s_' + k)) for k in self.E}
        self.cnt = {k: 0 for k in self.E}
        self.seen = {k: {} for k in self.E}
        self.lastw = {}
        self.rd = {}
        self.dpool = {}
        for q, n in (('sp', 24), ('pool', 8), ('act', 8)):
            self.dpool[q] = dict(sems=[es.enter_context(nc.semaphore(f'd_{q}{i}')) for i in range(n)],
                                 val=[0] * n, nxt=0)

    def _semof(self, key):
        return self.sem[key[1]] if key[0] == 'e' else self.dpool[key[1]]['sems'][key[2]]

    def _wait(self, e, key, val):
        if key == ('e', 'pe') and e == 'pe':
            return
        if self.seen[e].get(key, 0) >= val:
            return
        self.E[e].wait_ge(self._semof(key), val)
        self.seen[e][key] = val

    def _deps(self, e, reads, writes):
        for r in reads:
            w = self.lastw.get(r)
            if w:
                self._wait(e, *w)
        for x in writes:
            w = self.lastw.get(x)
            if w:
                self._wait(e, *w)
            for k, v in self.rd.get(x, {}).items():
                self._wait(e, k, v)

    def _record(self, tok, reads, writes):
        for r in reads:
            d = self.rd.setdefault(r, {})
            d[tok[0]] = max(d.get(tok[0], 0), tok[1])
        for x in writes:
            self.lastw[x] = tok
            self.rd[x] = {}

    def op(self, e, fn, reads=(), writes=()):
        self._deps(e, reads, writes)
        ins = fn(self.E[e])
        self.cnt[e] += 1
        ins.then_inc(self.sem[e], 1)
        self._record((('e', e), self.cnt[e]), reads, writes)

    def dma(self, q, out, in_, reads=(), writes=()):
        dp = self.dpool[q]
        i = dp['nxt']
        dp['nxt'] = (i + 1) % len(dp['sems'])
        if dp['val'][i] > 0:
            self._wait(q, ('d', q, i), dp['val'][i])
        self._deps(q, reads, writes)
        ins = self.E[q].dma_start(out=out, in_=in_)
        dp['val'][i] += 16
        ins.then_inc(dp['sems'][i], 16)
        self._record((('d', q, i), dp['val'][i]), reads, writes)

    def barrier(self):
        for e in self.E:
            for e2 in self.E:
                if e2 != e and self.cnt[e2] > 0:
                    self._wait(e, ('e', e2), self.cnt[e2])
            for q, dp in self.dpool.items():
                for i, v in enumerate(dp['val']):
                    if v > 0:
                        self._wait(e, ('d', q, i), v)
        self.lastw = {}
        self.rd = {}


def build_program(T, NS, QT=None):
    NPC = T // 512
    NT = T // 128
    QT = T if QT is None else QT
    HALO = 128 if QT < T else 0
    QPC = QT // 512
    qpieces = [(pc * 512, 512) for pc in range(QPC)] + ([(QT, 128)] if HALO else [])
    QX = QT + HALO
    nc = bass.Bass("TRN2", target_bir_lowering=False)

    def din(name, shape, dt=F32):
        return nc.dram_tensor(name, list(shape), dt, kind="ExternalInput").ap()

    xT_d = din("xT", [NS, D, T])
    x_d = din("x", [NS, QX, D])
    memT_d = din("memT", [NS, D, NMEM])
    wcat_d = din("wcat", [128, KC, NBLK * 128])
    wo_d = din("wo", [128, KC, D])
    wcq_d = din("wcq", [128, KC, D])
    wck_d = din("wck", [128, KC, D])
    wcv_d = din("wcv", [128, KC, D])
    wco_d = din("wco", [128, KC, D])
    wup_d = din("wup", [FC, 128, KC * 256])
    wdn_d = din("wdn", [128, FC, D])
    rope_d = din("rope", [NS, 128, 4, T])
    mask_d = din("mask", [128, NMASK, 512])
    lnv_d = din("lnv", [128, 6, D])
    cw_d = din("cw", [NS, 128, FC, 4])
    go_d = din("go", [128, KC])
    gq_d = din("gq", [128, 4])
    cst_d = din("cst", [128, 3, 128])
    sel_d = din("sel", [16, KC * 128])
    y_d = nc.dram_tensor("y", [NS, QT, D], F32, kind="ExternalOutput").ap()

    s_qa = nc.dram_tensor("s_qa", [4, 128, T], BF16).ap()
    s_ka = nc.dram_tensor("s_ka", [4, 128, T], BF16).ap()
    s_va = nc.dram_tensor("s_va", [4, 128, T], BF16).ap()
    s_qb = nc.dram_tensor("s_qb", [4, 128, T], BF16).ap()
    s_kb = nc.dram_tensor("s_kb", [2, 128, T], BF16).ap()
    s_vb = nc.dram_tensor("s_vb", [128, T], BF16).ap()
    s_mix = nc.dram_tensor("s_mix", [D, T], BF16).ap()
    s_x2t = nc.dram_tensor("s_x2t", [D, T], BF16).ap()
    s_x2 = nc.dram_tensor("s_x2", [T, D], F32).ap()
    s_wup = nc.dram_tensor("s_wup", [FC, 128, KC * 256], BF16).ap()
    s_den = nc.dram_tensor("s_den", [16, T], F32).ap()

    with ExitStack() as es:
        tk = Trk(nc, es)

        uid = [0]

        def sb(stack, name, shape, dt):
            uid[0] += 1
            return stack.enter_context(nc.sbuf_tensor(f"sb{uid[0]}_{name}", list(shape), dt))

        ps = es.enter_context(nc.psum_tensor("psum_all", [128, 8, 512], F32))

        def B(i):
            return ('ps', i)

        cstb = sb(es, "cstb", [128, 3, 128], BF16)
        cstf = sb(es, "cstf", [128, 512], F32)
        go = sb(es, "go", [128, KC], F32)
        gq = sb(es, "gq", [128, 4], F32)
        wo = sb(es, "wo", [128, KC, D], BF16)
        epsc = sb(es, "epsc", [128, 4], F32)
        tk.dma('pool', cstb[:], cst_d, writes=['cstb'])
        tk.dma('sp', go[:], go_d, writes=['go'])
        tk.dma('sp', gq[:], gq_d, writes=['gq'])
        tk.op('dve', lambda e: e.memset(cstf[:], 1.0), writes=['cstf'])
        cneg = sb(es, "cneg", [128, 512], F32)
        tk.op('dve', lambda e: e.memset(cneg[:], -1.0), writes=['cneg'])
        tk.op('dve', lambda e: e.memset(epsc[:, 0:1], LN_EPS), writes=['epsc0'])
        tk.op('dve', lambda e: e.memset(epsc[:, 1:2], RMS_EPS), writes=['epsc1'])
        tk.op('dve', lambda e: e.memset(epsc[:, 2:3], 64.0 * RMS_EPS), writes=['epsc2'])
        tk.op('dve', lambda e: e.memset(epsc[:, 3:4], 0.0), writes=['epsc3'])
        IDB = cstb[:, 0, :]
        BDB = cstb[:, 1, :]
        ONB = cstb[:, 2, :]
        with ExitStack() as st0:
            wof = sb(st0, "wof", [128, KC, D], F32)
            tk.dma('sp', wof[:], wo_d, writes=['wof'])
            for kc in range(KC):
                tk.op('dve', lambda e, kc=kc: e.tensor_scalar(out=wo[:, kc, :], in0=wof[:, kc, :],
                                                              scalar1=go[:, kc:kc + 1], scalar2=None, op0=ALU.mult),
                      reads=['wof', 'go'], writes=[('wo', kc)])
            wst = [sb(st0, f"wst{i}", [128, KC * 256], BF16) for i in range(2)]
            for fc in range(FC):
                tk.dma('pool', wst[fc % 2][:], wup_d[fc], writes=[('wst', fc % 2)])
                tk.dma('sp', s_wup[fc], wst[fc % 2][:], reads=[('wst', fc % 2)])
            tk.barrier()

        def mm(out, lhsT, rhs, start, stop, reads, writes):
            tk.op('pe', lambda e: e.matmul(out, lhsT, rhs, start=start, stop=stop), reads=reads, writes=writes)

        def layer_norm(stack_bufs, src_ap, src_key, g_ap, b_ap, out_ap, out_key, tagi):
            stt, mv, rstd = stack_bufs
            k = ('ln', tagi)
            tk.op('dve', lambda e: e.bn_stats(stt[:, 0:6], src_ap[:, 0:512]), reads=[src_key], writes=[(k, 's0')])
            tk.op('dve', lambda e: e.bn_stats(stt[:, 6:12], src_ap[:, 512:1024]), reads=[src_key], writes=[(k, 's1')])
            tk.op('dve', lambda e: e.bn_aggr(mv[:], stt[:]), reads=[(k, 's0'), (k, 's1')], writes=[(k, 'mv')])
            tk.op('act', lambda e: e.activation(out=rstd[:, 0:1], in_=mv[:, 1:2], func=AF.Sqrt, bias=epsc[:, 0:1], scale=1.0),
                  reads=[(k, 'mv'), 'epsc0'], writes=[(k, 'sd')])
            tk.op('dve', lambda e: e.reciprocal(out=rstd[:, 1:2], in_=rstd[:, 0:1]), reads=[(k, 'sd')], writes=[(k, 'rs')])
            tk.op('dve', lambda e: e.scalar_tensor_tensor(out=src_ap, in0=src_ap, scalar=mv[:, 0:1], in1=g_ap, op0=ALU.subtract, op1=ALU.mult),
                  reads=[src_key, (k, 'mv'), 'lnv'], writes=[src_key])
            tk.op('dve', lambda e: e.scalar_tensor_tensor(out=out_ap, in0=src_ap, scalar=rstd[:, 1:2], in1=b_ap, op0=ALU.mult, op1=ALU.add),
                  reads=[src_key, (k, 'rs'), 'lnv'], writes=[out_key])

        for s in range(NS):
            xTv = xT_d[s].rearrange("(kc p) t -> p kc t", p=128)
            with ExitStack() as st:
                wcat = sb(st, "wcat", [128, KC, NBLK * 128], BF16)
                xt = [sb(st, f"xt{i}", [128, KC, 512], BF16) for i in range(2)]
                tb = [sb(st, f"tb{i}", [128, 4, 512], F32) for i in range(2)]
                gt = sb(st, "gt", [128, 4, 512], F32)
                t1 = [sb(st, f"t1_{i}", [128, 512], F32) for i in range(2)]
                t2 = [sb(st, f"t2_{i}", [128, 512], F32) for i in range(2)]
                rs = [sb(st, f"rs_{i}", [128, 512], F32) for i in range(2)]
                sq = [sb(st, f"sq_{i}", [128, 512], BF16) for i in range(2)]
                og = [sb(st, f"og_{i}", [128, 512], BF16) for i in range(4)]
                for kc in range(KC):
                    pass
                WCH = [(0, 8), (8, 16), (16, 24), (24, 33)]
                for ci, (b_lo, b_hi) in enumerate(WCH):
                    tk.dma('pool', wcat[:, :, b_lo * 128:b_hi * 128], wcat_d[:, :, b_lo * 128:b_hi * 128], writes=[('wcat', ci)])

                def wcat_key(blk):
                    for ci, (b_lo, b_hi) in enumerate(WCH):
                        if b_lo <= blk < b_hi:
                            return ('wcat', ci)
                cnt = dict(bank=0, t=0, og=0)

                def nbank():
                    b = cnt['bank']
                    cnt['bank'] = (b + 1) % 8
                    return b

                def proj(blk, xi, bank):
                    for kc in range(KC):
                        mm(ps[:, bank, :], wcat[:, kc, blk * 128:(blk + 1) * 128], xt[xi][:, kc, :],
                           kc == 0, kc == KC - 1, reads=[('xt', xi)] + ([wcat_key(blk)] if kc == 0 else []), writes=[B(bank)])

                def store(oi, dst):
                    tk.dma('sp', dst, og[oi][:], reads=[('og', oi)])

                def next_og():
                    o = cnt['og']
                    cnt['og'] = (o + 1) % 4
                    return o

                def load_piece(pc):
                    i = pc % 2
                    t0 = pc * 512
                    tk.dma('pool', xt[i][:], xTv[:, :, t0:t0 + 512], writes=[('xt', i)])
                    tk.dma('sp', tb[i][:], rope_d[s, :, :, t0:t0 + 512], writes=[('tb', i)])

                load_piece(0)
                for pc in range(NPC):
                    i = pc % 2
                    t0 = pc * 512
                    if pc + 1 < NPC:
                        load_piece(pc + 1)
                    for j, (tbi, gcol) in enumerate(((2, 0), (3, 1), (2, 2), (3, 3))):
                        tk.op('pool', lambda e, j=j, tbi=tbi, gcol=gcol: e.tensor_scalar(
                            out=gt[:, j, :], in0=tb[i][:, tbi, :], scalar1=gq[:, gcol:gcol + 1], scalar2=8.0,
                            op0=ALU.mult, op1=ALU.mult), reads=[('tb', i), 'gq'], writes=[('gt', j)])
                    need_q = t0 < QX
                    need_ka = t0 < QT + 1536
                    for (b0, bs, dst) in ([(QA, QAS, s_qa)] if need_q else []) + ([(KA, KAS, s_ka)] if need_ka else []):
                        for j in range(4):
                            bx, by = nbank(), nbank()
                            proj(b0 + j, i, bx)
                            proj(bs + j, i, by)
                            ti = cnt['t']
                            cnt['t'] = (ti + 1) % 2
                            tk.op('dve', lambda e, bx=bx, ti=ti: e.tensor_tensor(out=t1[ti][:], in0=ps[:, bx, :], in1=tb[i][:, 0, :], op=ALU.mult),
                                  reads=[B(bx), ('tb', i)], writes=[('t1', ti)])
                            tk.op('dve', lambda e, by=by, ti=ti: e.tensor_tensor(out=t2[ti][:], in0=ps[:, by, :], in1=tb[i][:, 1, :], op=ALU.mult),
                                  reads=[B(by), ('tb', i)], writes=[('t2', ti)])
                            oi = next_og()
                            tk.op('pool', lambda e, ti=ti, oi=oi: e.tensor_tensor(out=og[oi][:], in0=t1[ti][:], in1=t2[ti][:], op=ALU.add),
                                  reads=[('t1', ti), ('t2', ti)], writes=[('og', oi)])
                            store(oi, dst[j, :, t0:t0 + 512])
                    for (blk, dst) in ([(VA + j, s_va[j, :, t0:t0 + 512]) for j in range(4)] if need_ka else []) + [(VB, s_vb[:, t0:t0 + 512])]:
                        bx = nbank()
                        proj(blk, i, bx)
                        oi = next_og()
                        tk.op('act', lambda e, bx=bx, oi=oi: e.activation(out=og[oi][:], in_=ps[:, bx, :], func=AF.Copy),
                              reads=[B(bx)], writes=[('og', oi)])
                        store(oi, dst)
                    pairs = ([(QB + j, QBS + j, 0, s_qb[j, :, t0:t0 + 512]) for j in range(4)] if need_q else []) + \
                            [(KB0, KB0S, 2, s_kb[0, :, t0:t0 + 512]), (KB1, KB1S, 2, s_kb[1, :, t0:t0 + 512])]
                    for (bq, bqs, gj, dst) in pairs:
                        bx, by, bz = nbank(), nbank(), nbank()
                        proj(bq, i, bx)
                        proj(bqs, i, by)
                        ti = cnt['t']
                        cnt['t'] = (ti + 1) % 2
                        tk.op('act', lambda e, bx=bx, ti=ti: e.activation(out=sq[ti][:], in_=ps[:, bx, :], func=AF.Square),
                              reads=[B(bx)], writes=[('sq', ti)])
                        mm(ps[:, bz, :], BDB, sq[ti][:], True, True, reads=[('sq', ti), 'cstb'], writes=[B(bz)])
                        tk.op('act', lambda e, bz=bz, ti=ti: e.activation(out=t2[ti][:], in_=ps[:, bz, :], func=AF.Sqrt, bias=epsc[:, 2:3], scale=1.0),
                              reads=[B(bz), 'epsc2'], writes=[('t2', ti)])
                        tk.op('dve', lambda e, ti=ti: e.reciprocal(out=rs[ti][:], in_=t2[ti][:]), reads=[('t2', ti)], writes=[('rs', ti)])
                        tk.op('dve', lambda e, bx=bx, ti=ti, gj=gj: e.tensor_tensor(out=t1[ti][:], in0=ps[:, bx, :], in1=gt[:, gj, :], op=ALU.mult),
                              reads=[B(bx), ('gt', gj)], writes=[('t1', ti)])
                        tk.op('dve', lambda e, by=by, ti=ti, gj=gj: e.tensor_tensor(out=t2[ti][:], in0=ps[:, by, :], in1=gt[:, gj + 1, :], op=ALU.mult),
                              reads=[B(by), ('gt', gj + 1)], writes=[('t2', ti)])
                        tk.op('pool', lambda e, ti=ti: e.tensor_tensor(out=t1[ti][:], in0=t1[ti][:], in1=t2[ti][:], op=ALU.add),
                              reads=[('t1', ti), ('t2', ti)], writes=[('t1', ti)])
                        oi = next_og()
                        tk.op('pool', lambda e, ti=ti, oi=oi: e.tensor_tensor(out=og[oi][:], in0=t1[ti][:], in1=rs[ti][:], op=ALU.mult),
                              reads=[('t1', ti), ('rs', ti)], writes=[('og', oi)])
                        store(oi, dst)
                tk.barrier()

            def attention(st, tag, jobs, use_mask, mk):
                qzs = [[sb(st, tag + f"qz{jb}_{i}", [128, T], BF16) for i in range(2)] for jb in range(2)]
                ksbs = [sb(st, tag + f"k{jb}", [128, T], BF16) for jb in range(2)]
                vsb = sb(st, tag + "v", [128, T], BF16)
                vtm = sb(st, tag + "vtm", [128, NT, 2, 128], BF16)
                pt = [sb(st, tag + f"pt{i}", [128, 3, 512], BF16) for i in range(4)]
                rden = [sb(st, tag + f"rd{i}", [128, 512], F32) for i in range(2)]
                ost = [sb(st, tag + f"os{i}", [128, 512], BF16) for i in range(2)]
                tk.op('pool', lambda e: e.memset(vtm[:], 1.0), writes=['vtm'])
                for jb in range(2):
                    tk.op('pool', lambda e, jb=jb: e.memset(qzs[jb][0][64:128, :], 0.0), writes=[('qzpad', jb, 0)])
                    tk.op('pool', lambda e, jb=jb: e.memset(qzs[jb][1][0:64, :], 0.0), writes=[('qzpad', jb, 1)])

                def load_job(ji):
                    jb = ji % 2
                    q_src, k_src = jobs[ji][0], jobs[ji][1]
                    tk.dma('sp', qzs[jb][0][0:64, :], q_src[0:64, :], writes=[('qz', jb, 0)])
                    tk.dma('sp', qzs[jb][1][64:128, :], q_src[64:128, :], writes=[('qz', jb, 64)])
                    tk.dma('sp', ksbs[jb][:], k_src, writes=[('ksb', jb)])

                load_job(0)
                c = dict(g=0, p=0, o=0, m=0)
                last = dict(k=None, v=None)
                for ji, (q_src, k_src, v_src, heads) in enumerate(jobs):
                    jb = ji % 2
                    qz = qzs[jb]
                    ksb = ksbs[jb]
                    if ji + 1 < len(jobs):
                        load_job(ji + 1)
                    if last['v'] is not v_src:
                        tk.dma('sp', vsb[:], v_src, writes=['vsb'])
                        last['v'] = v_src
                        for t4 in range(0, NT, 4):
                            bk = 6 + (t4 // 4) % 2
                            pb = ps[:, bk, :].bitcast(BF16)
                            n4 = min(4, NT - t4)
                            for u in range(n4):
                                tk.op('pe', lambda e, u=u, t4=t4, pb=pb: e.transpose(pb[:, u * 128:(u + 1) * 128], vsb[:, (t4 + u) * 128:(t4 + u + 1) * 128], IDB),
                                      reads=['vsb', 'cstb'], writes=[B(bk)])
                            tk.op('act', lambda e, t4=t4, n4=n4, pb=pb: e.activation(
                                out=vtm[:, t4:t4 + n4, :, 0:64],
                                in_=pb[:, 0:n4 * 128].rearrange("p (a h e) -> p a h e", h=2, e=64), func=AF.Copy),
                                reads=[B(bk)], writes=['vtm'])
                    tasks = []
                    for (q0, qw) in qpieces:
                        if use_mask:
                            tiles = [(j, (q0 - 1024) // 128 + j) for j in range(NMASK)]
                            tiles = [(j, kt) for (j, kt) in tiles if 0 <= kt < NT]
                        else:
                            tiles = [(None, kt) for kt in range(NT)]
                        groups = [tiles[a:a + 3] for a in range(0, len(tiles), 3)]
                        for (r0, vh, mrow) in heads:
                            for gi, grp in enumerate(groups):
                                tasks.append(dict(q0=q0, w=qw, r0=r0, vh=vh, mrow=mrow, grp=grp, first=(gi == 0), last=(gi == len(groups) - 1)))

                    def emit_s(t):
                        g3 = (c['g'] % 2) * 3
                        c['g'] += 1
                        t['g3'] = g3
                        r0, q0, w = t['r0'], t['q0'], t['w']
                        for u, (j, kt) in enumerate(t['grp']):
                            mm(ps[:, g3 + u, 0:w], ksb[:, kt * 128:(kt + 1) * 128], qz[r0 // 64][:, q0:q0 + w],
                               True, True, reads=[('ksb', jb), ('qz', jb, r0), ('qzpad', jb, r0 // 64)], writes=[B(g3 + u)])

                    def emit_exp(t):
                        g3 = t['g3']
                        grp = t['grp']
                        n = len(grp)
                        pi = c['p'] % 4
                        c['p'] += 1
                        t['pi'] = pi
                        w = t['w']
                        tk.op('act', lambda e: e.activation(out=pt[pi][:, 0:n, 0:w], in_=ps[:, g3:g3 + n, 0:w], func=AF.Exp, scale=0.125),
                              reads=[B(g3 + u) for u in range(n)], writes=[('pt', pi)])
                        if use_mask:
                            j0 = grp[0][0]
                            tk.op('dve', lambda e: e.tensor_tensor(out=pt[pi][:, 0:n, 0:w], in0=pt[pi][:, 0:n, 0:w], in1=mk[:, j0:j0 + n, 0:w], op=ALU.mult),
                                  reads=[('pt', pi), 'mk'], writes=[('pt', pi)])

                    def emit_pv(t):
                        grp = t['grp']
                        n = len(grp)
                        pi = t['pi']
                        if t['first']:
                            c['o'] += 1
                        ob = 6 + c['o'] % 2
                        oi = c['o'] % 2
                        w = t['w']
                        for u, (j, kt) in enumerate(grp):
                            mm(ps[:, ob, 0:w], vtm[:, kt, t['vh'], :], pt[pi][:, u, 0:w], t['first'] and u == 0, t['last'] and u == n - 1,
                               reads=[('pt', pi), 'vtm'], writes=[B(ob)])
                        return ob, oi

                    def tail(ob, oi, mrow, q0, w):
                        ceng = 'act' if use_mask else 'dve'
                        if ceng == 'act':
                            tk.op('act', lambda e: e.activation(out=ost[oi][0:64, 0:w], in_=ps[0:64, ob, 0:w], func=AF.Copy),
                                  reads=[B(ob)], writes=[('ost', oi)])
                        else:
                            tk.op('dve', lambda e: e.tensor_copy(out=ost[oi][0:64, 0:w], in_=ps[0:64, ob, 0:w]),
                                  reads=[B(ob)], writes=[('ost', oi)])
                        tk.op('dve', lambda e: e.tensor_copy(out=rden[oi][64:65, 0:w], in_=ps[64:65, ob, 0:w]),
                              reads=[B(ob)], writes=[('rden', oi)])
                        tk.dma('sp', s_mix[mrow:mrow + 64, q0:q0 + w], ost[oi][0:64, 0:w], reads=[('ost', oi)])
                        hd = mrow // 64
                        tk.dma('sp', s_den[hd:hd + 1, q0:q0 + w], rden[oi][64:65, 0:w], reads=[('rden', oi)])

                    emit_s(tasks[0])
                    if len(tasks) > 1:
                        emit_s(tasks[1])
                    pend = []
                    for ti, t in enumerate(tasks):
                        emit_exp(t)
                        for pdn in pend:
                            pdn[0] -= 1
                        while pend and pend[0][0] <= 0:
                            tail(*pend.pop(0)[1])
                        if ti + 2 < len(tasks):
                            emit_s(tasks[ti + 2])
                        ob, oi = emit_pv(t)
                        if t['last']:
                            pend.append([2, (ob, oi, t['mrow'], t['q0'], t['w'])])
                    while pend:
                        tail(*pend.pop(0)[1])

            with ExitStack() as st:
                mk = sb(st, "mk", [128, NMASK, 512], BF16)
                tk.dma('pool', mk[:], mask_d, writes=['mk'])
                jobs = []
                for hp in range(4):
                    jobs.append((s_qa[hp], s_ka[hp], s_va[hp],
                                 [(0, 0, hp * 128), (64, 1, hp * 128 + 64)]))
                attention(st, "a", jobs, True, mk)
                tk.barrier()
            with ExitStack() as st:
                jobs = []
                for hp in range(4):
                    g = hp // 2
                    jobs.append((s_qb[hp], s_kb[g], s_vb,
                                 [(0, g, 512 + hp * 128), (64, g, 512 + hp * 128 + 64)]))
                attention(st, "b", jobs, False, None)
                tk.barrier()

            with ExitStack() as st:
                wcq = sb(st, "wcq", [128, KC, D], BF16)
                wco = sb(st, "wco", [128, KC, D], BF16)
                km = sb(st, "km", [128, KC, NMEM], BF16)
                vm = sb(st, "vm", [128, 2, D], BF16)
                lnv = sb(st, "lnv", [128, 4, D], F32)
                mxs = [sb(st, f"mx{i}", [128, KC, 512], BF16) for i in range(2)]
                sqms = [sb(st, f"sqm{i}", [128, KC, 512], BF16) for i in range(2)]
                xtm = [sb(st, f"xtm{i}", [128, D], F32) for i in range(2)]
                yb = sb(st, "yb", [128, D], F32)
                x1 = sb(st, "x1", [128, 4, D], F32)
                xbf = [sb(st, f"xbf{i}", [128, D], BF16) for i in range(2)]
                x1t = sb(st, "x1t", [128, KC, 512], BF16)
                qc = sb(st, "qc", [128, KC, 512], BF16)
                ptc = [sb(st, f"ptc{i}", [128, 2, 512], BF16) for i in range(2)]
                rdc = [sb(st, f"rdc{i}", [128, 512], F32) for i in range(2)]
                oc = sb(st, "oc", [128, KC, 512], BF16)
                x2 = [sb(st, f"x2_{i}", [128, D], F32) for i in range(2)]
                x2ts = [sb(st, f"x2ts{i}", [128, KC, 128], BF16) for i in range(2)]
                rsabs = [sb(st, f"rsab{i}", [128, 16], F32) for i in range(2)]
                lnb = (sb(st, "lnst", [128, 12], F32), sb(st, "lnmv", [128, 2], F32), sb(st, "lnrs", [128, 2], F32))
                st_in = ExitStack()
                wtmp = sb(st_in, "wtmp", [128, KC, D], BF16)
                memt = sb(st_in, "memt", [128, KC, NMEM], BF16)
                tk.dma('pool', wcq[:], wcq_d, writes=['wcq'])
                tk.dma('pool', wco[:], wco_d, writes=['wco'])
                tk.dma('sp', lnv[:], lnv_d[:, 0:4, :], writes=['lnv'])
                tk.dma('pool', memt[:], memT_d[s].rearrange("(kc p) m -> p kc m", p=128), writes=['memt'])
                tk.dma('pool', wtmp[:], wck_d, writes=['wtmp'])
                for cb in range(KC):
                    bk = cb % 4
                    for kc in range(KC):
                        mm(ps[:, bk, 0:NMEM], wtmp[:, kc, cb * 128:(cb + 1) * 128], memt[:, kc, :], kc == 0, kc == KC - 1,
                           reads=['wtmp', 'memt'], writes=[B(bk)])
                    tk.op('act', lambda e, bk=bk, cb=cb: e.activation(out=km[:, cb, :], in_=ps[:, bk, 0:NMEM], func=AF.Copy),
                          reads=[B(bk)], writes=['km'])
                tk.dma('pool', wtmp[:], wcv_d, reads=[], writes=['wtmp'])
                for mt in range(2):
                    for nb in range(2):
                        bk = 4 + (mt * 2 + nb) % 4
                        for kc in range(KC):
                            mm(ps[:, bk, :], memt[:, kc, mt * 128:(mt + 1) * 128], wtmp[:, kc, nb * 512:(nb + 1) * 512], kc == 0, kc == KC - 1,
                               reads=['wtmp', 'memt'], writes=[B(bk)])
                        tk.op('act', lambda e, bk=bk, mt=mt, nb=nb: e.activation(out=vm[:, mt, nb * 512:(nb + 1) * 512], in_=ps[:, bk, :], func=AF.Copy),
                              reads=[B(bk)], writes=['vm'])

                tk.barrier()
                st_in.close()
                dn16 = [sb(st, f"dn16_{i}", [16, 512], F32) for i in range(2)]
                rd16 = [sb(st, f"rd16_{i}", [16, 512], F32) for i in range(2)]
                selc = sb(st, "selc", [16, KC * 128], F32)
                tk.dma('sp', selc[:], sel_d, writes=['selc'])
                mixv = s_mix.rearrange("(kc p) t -> p kc t", p=128)
                x2tv = s_x2t.rearrange("(kc p) t -> p kc t", p=128)
                lnc = 0
                def prep_load(qi):
                    t0, W = qpieces[qi]
                    m = qi % 2
                    tk.dma('sp', mxs[m][:, :, 0:W], mixv[:, :, t0:t0 + W], writes=[('mx', m)])
                    tk.dma('sp', dn16[m][:, 0:W], s_den[:, t0:t0 + W], writes=[('dn16', m)])

                def prep_norm(qi):
                    t0, W = qpieces[qi]
                    m = qi % 2
                    tk.op('dve', lambda e: e.reciprocal(out=rd16[m][:, 0:W], in_=dn16[m][:, 0:W]), reads=[('dn16', m)], writes=[('rd16', m)])
                    for kc in range(KC):
                        bk = 6 + kc % 2
                        mm(ps[:, bk, 0:W], selc[:, kc * 128:(kc + 1) * 128], rd16[m][:, 0:W], True, True,
                           reads=[('rd16', m), 'selc'], writes=[B(bk)])
                        tk.op('dve', lambda e, kc=kc, bk=bk: e.tensor_tensor(out=mxs[m][:, kc, 0:W], in0=mxs[m][:, kc, 0:W], in1=ps[:, bk, 0:W], op=ALU.mult),
                              reads=[B(bk), ('mx', m)], writes=[('mx', m)])
                    tk.op('pool', lambda e: e.tensor_tensor(out=sqms[m][:, :, 0:W], in0=mxs[m][:, :, 0:W], in1=mxs[m][:, :, 0:W], op=ALU.mult),
                          reads=[('mx', m)], writes=[('sqm', m)])

                def prep_ssq(qi):
                    t0, W = qpieces[qi]
                    m = qi % 2
                    for tt in range(W // 128):
                        for half in range(2):
                            col = tt * 2 + half
                            for k4 in range(4):
                                kc = half * 4 + k4
                                mm(ps[:, 7, col:col + 1], sqms[m][:, kc, tt * 128:(tt + 1) * 128], ONB[:, 0:1], k4 == 0, k4 == 3,
                                   reads=[('sqm', m), 'cstb'], writes=[B(7)])
                    tk.op('act', lambda e: e.activation(out=rsabs[m][:, 0:8], in_=ps[:, 7, 0:8], func=AF.Sqrt, bias=epsc[:, 1:2], scale=1.0 / 512.0),
                          reads=[B(7), 'epsc1'], writes=[('rsab0', m)])
                    tk.op('dve', lambda e: e.reciprocal(out=rsabs[m][:, 8:16], in_=rsabs[m][:, 0:8]), reads=[('rsab0', m)], writes=[('rsab', m)])

                prep_load(0)
                prep_norm(0)
                prep_ssq(0)
                for qi, (t0, W) in enumerate(qpieces):
                    NTT = W // 128
                    mx = mxs[qi % 2]
                    rsab = rsabs[qi % 2]
                    mkey = ('mx', qi % 2)
                    rkey = ('rsab', qi % 2)
                    if qi + 1 < len(qpieces):
                        prep_load(qi + 1)
                    pend1 = []
                    for tt in range(NTT):
                        xi = tt % 2
                        tk.dma('sp', xtm[xi][:], x_d[s, t0 + tt * 128:t0 + (tt + 1) * 128, :], writes=[('xtm', xi)])
                        for half in range(2):
                            for nb in range(2):
                                bk = half * 2 + nb
                                for k4 in range(4):
                                    kc = half * 4 + k4
                                    mm(ps[:, bk, :], mx[:, kc, tt * 128:(tt + 1) * 128], wo[:, kc, nb * 512:(nb + 1) * 512], k4 == 0, k4 == 3,
                                       reads=[mkey] + [('wo', kc)], writes=[B(bk)])
                        while pend1:
                            pend1.pop(0)()
                        ya = ps[:, 0:2, :].rearrange("p a b -> p (a b)")
                        ybm = ps[:, 2:4, :].rearrange("p a b -> p (a b)")
                        ca = 8 + tt * 2
                        tk.op('dve', lambda e, ya=ya, ca=ca: e.tensor_scalar(out=yb[:], in0=ya, scalar1=rsab[:, ca:ca + 1], scalar2=None, op0=ALU.mult),
                              reads=[B(0), B(1), rkey], writes=['yb'])
                        tk.op('dve', lambda e, ybm=ybm, ca=ca: e.scalar_tensor_tensor(out=yb[:], in0=ybm, scalar=rsab[:, ca + 1:ca + 2], in1=yb[:], op0=ALU.mult, op1=ALU.add),
                              reads=[B(2), B(3), rkey, 'yb'], writes=['yb'])
                        tk.op('dve', lambda e, xi=xi: e.scalar_tensor_tensor(out=yb[:], in0=xtm[xi][:], scalar=ALPHA, in1=yb[:], op0=ALU.mult, op1=ALU.add),
                              reads=[('xtm', xi), 'yb'], writes=['yb'])
                        layer_norm(lnb, yb[:], 'yb', lnv[:, 0, :], lnv[:, 1, :], x1[:, tt, :], ('x1', tt), 0)
                        xb_i = tt % 2
                        tk.op('act', lambda e, tt=tt, xb_i=xb_i: e.activation(out=xbf[xb_i][:], in_=x1[:, tt, :], func=AF.Copy), reads=[('x1', tt)], writes=[('xbf', xb_i)])

                        def tr1(tt=tt, xb_i=xb_i):
                            bkt = 4 + xb_i
                            pb = ps[:, bkt, :].bitcast(BF16)
                            for kc in range(KC):
                                tk.op('pe', lambda e, kc=kc: e.transpose(pb[:, kc * 128:(kc + 1) * 128], xbf[xb_i][:, kc * 128:(kc + 1) * 128], IDB),
                                      reads=[('xbf', xb_i), 'cstb'], writes=[B(bkt)])
                            tk.op('act', lambda e: e.activation(out=x1t[:, :, tt * 128:(tt + 1) * 128],
                                                                in_=pb.rearrange("p (k t) -> p k t", k=KC), func=AF.Copy),
                                  reads=[B(bkt)], writes=['x1t'])
                        pend1.append(tr1)
                    while pend1:
                        pend1.pop(0)()
                    if qi + 1 < len(qpieces):
                        prep_norm(qi + 1)
                    for cb in range(KC):
                        bk = cb % 4
                        for kc in range(KC):
                            mm(ps[:, bk, 0:W], wcq[:, kc, cb * 128:(cb + 1) * 128], x1t[:, kc, 0:W], kc == 0, kc == KC - 1,
                               reads=['wcq', 'x1t'], writes=[B(bk)])
                        tk.op('act', lambda e, bk=bk, cb=cb: e.activation(out=qc[:, cb, 0:W], in_=ps[:, bk, 0:W], func=AF.Copy),
                              reads=[B(bk)], writes=[('qc', cb)])
                    if qi + 1 < len(qpieces):
                        prep_ssq(qi + 1)

                    def ca_x(hc):
                        pi = hc % 2
                        sbk = (4, 5) if pi == 0 else (2, 3)
                        dbk = 6 if pi == 0 else 1
                        for mt in range(2):
                            for j in range(2):
                                mm(ps[:, sbk[mt], 0:W], km[:, 2 * hc + j, mt * 128:(mt + 1) * 128], qc[:, 2 * hc + j, 0:W], j == 0, j == 1,
                                   reads=['km', ('qc', 2 * hc + j)], writes=[B(sbk[mt])])
                        tk.op('act', lambda e: e.activation(out=ptc[pi][:, :, 0:W], in_=ps[:, sbk[0]:sbk[0] + 2, 0:W], func=AF.Exp, scale=1.0 / 16.0),
                              reads=[B(sbk[0]), B(sbk[1])], writes=[('ptc', pi)])
                        for mt in range(2):
                            mm(ps[:, dbk, 0:W], ONB, ptc[pi][:, mt, 0:W], mt == 0, mt == 1, reads=[('ptc', pi), 'cstb'], writes=[B(dbk)])
                        tk.op('dve', lambda e: e.reciprocal(out=rdc[pi][:, 0:W], in_=ps[:, dbk, 0:W]), reads=[B(dbk)], writes=[('rdc', pi)])

                    def ca_y(hc):
                        pi = hc % 2
                        pvb = 7 if pi == 0 else 0
                        for j in range(2):
                            for mt in range(2):
                                mm(ps[:, pvb, 0:W], vm[:, mt, hc * 256 + j * 128:hc * 256 + (j + 1) * 128], ptc[pi][:, mt, 0:W], mt == 0, mt == 1,
                                   reads=[('ptc', pi), 'vm'], writes=[B(pvb)])
                            tk.op('dve', lambda e, j=j: e.tensor_tensor(out=oc[:, 2 * hc + j, 0:W], in0=ps[:, pvb, 0:W], in1=rdc[pi][:, 0:W], op=ALU.mult),
                                  reads=[B(pvb), ('rdc', pi)], writes=[('oc', 2 * hc + j)])

                    ca_x(0)
                    for hc in range(4):
                        if hc + 1 < 4:
                            ca_x(hc + 1)
                        ca_y(hc)
                    pend2 = []
                    for tt in range(NTT):
                        for nb in range(2):
                            for kc in range(KC):
                                mm(ps[:, nb, :], oc[:, kc, tt * 128:(tt + 1) * 128], wco[:, kc, nb * 512:(nb + 1) * 512], kc == 0, kc == KC - 1,
                                   reads=['wco'] + [('oc', kc)], writes=[B(nb)])
                        while pend2:
                            pend2.pop(0)()
                        ya = ps[:, 0:2, :].rearrange("p a b -> p (a b)")
                        tk.op('dve', lambda e, tt=tt, ya=ya: e.scalar_tensor_tensor(out=yb[:], in0=x1[:, tt, :], scalar=ALPHA, in1=ya, op0=ALU.mult, op1=ALU.add),
                              reads=[('x1', tt), B(0), B(1)], writes=['yb'])
                        xi = tt % 2
                        layer_norm(lnb, yb[:], 'yb', lnv[:, 2, :], lnv[:, 3, :], x2[xi][:], ('x2', xi), 1)
                        tk.dma('sp', s_x2[t0 + tt * 128:t0 + (tt + 1) * 128, :], x2[xi][:], reads=[('x2', xi)])
                        tk.op('act', lambda e, xi=xi: e.activation(out=xbf[xi][:], in_=x2[xi][:], func=AF.Copy), reads=[('x2', xi)], writes=[('xbf', xi)])

                        def tr2(tt=tt, xi=xi):
                            bkt = 4 + xi
                            pb = ps[:, bkt, :].bitcast(BF16)
                            for kc in range(KC):
                                tk.op('pe', lambda e, kc=kc: e.transpose(pb[:, kc * 128:(kc + 1) * 128], xbf[xi][:, kc * 128:(kc + 1) * 128], IDB),
                                      reads=[('xbf', xi), 'cstb'], writes=[B(bkt)])
                            tk.op('act', lambda e: e.activation(out=x2ts[xi][:], in_=pb.rearrange("p (k t) -> p k t", k=KC), func=AF.Copy),
                                  reads=[B(bkt)], writes=[('x2ts', xi)])
                            tk.dma('sp', x2tv[:, :, t0 + tt * 128:t0 + (tt + 1) * 128], x2ts[xi][:], reads=[('x2ts', xi)])
                        pend2.append(tr2)
                    while pend2:
                        pend2.pop(0)()
                tk.barrier()

            with ExitStack() as st:
                wdn = sb(st, "wdn", [128, FC, D], BF16)
                wup = [sb(st, f"wup{i}", [128, KC, 256], BF16) for i in range(3)]
                xh = [sb(st, f"xh{i}", [128, KC, 514], BF16) for i in range(2)]
                ab = [sb(st, f"ab{i}", [128, 512], F32) for i in range(2)]
                ge = [sb(st, f"ge{i}", [128, 512], F32) for i in range(2)]
                hl = [sb(st, f"hl{i}", [128, 2], F32) for i in range(2)]
                vb = [sb(st, f"vb{i}", [128, 512], F32) for i in range(2)]
                ht = sb(st, "ht", [128, FC, 512], BF16)
                lnv3 = sb(st, "lnv3", [128, 2, D], F32)
                x2r = [sb(st, f"x2r{i}", [128, D], F32) for i in range(2)]
                y3 = [sb(st, f"y3_{i}", [128, D], F32) for i in range(2)]
                yo = [sb(st, f"yo_{i}", [128, D], F32) for i in range(2)]
                lnb = (sb(st, "lnst3", [128, 12], F32), sb(st, "lnmv3", [128, 2], F32), sb(st, "lnrs3", [128, 2], F32))
                cw = sb(st, "cws", [128, FC, 4], F32)
                tk.dma('sp', cw[:], cw_d[s], writes=['cw'])
                tk.dma('sp', lnv3[:], lnv_d[:, 4:6, :], writes=['lnv'])
                x2tv = s_x2t.rearrange("(kc p) t -> p kc t", p=128)
                wc = 0
                def load_xh(pc):
                    t0 = pc * 512
                    xi = pc % 2
                    lo = 1 if pc == 0 else 0
                    hi = 513 if (pc == QPC - 1 and not HALO) else 514
                    if pc == 0:
                        tk.op('pool', lambda e: e.memset(xh[xi][:, :, 0:1], 0.0), writes=[('xh', xi)])
                    if pc == QPC - 1 and not HALO:
                        tk.op('pool', lambda e: e.memset(xh[xi][:, :, 513:514], 0.0), writes=[('xh', xi)])
                    for kc in range(KC):
                        tk.dma('sp', xh[xi][:, kc, lo:hi], x2tv[:, kc, t0 - 1 + lo:t0 - 1 + hi], writes=[('xh', xi)])

                load_xh(0)
                for pc in range(QPC):
                    t0 = pc * 512
                    xi = pc % 2
                    if pc + 1 < QPC:
                        load_xh(pc + 1)
                    for fc in range(FC):
                        if pc == 0:
                            tk.dma('pool', wdn[:, fc, :], wdn_d[:, fc, :], writes=[('wdn', fc)])
                        wi = wc % 3
                        if wc == 0:
                            for pf in range(2):
                                tk.dma('act', wup[pf][:].rearrange("p k c -> p (k c)"), s_wup[pf], writes=[('wup', pf)])
                        nxt = wc + 2
                        if nxt < QPC * FC:
                            tk.dma('act', wup[nxt % 3][:].rearrange("p k c -> p (k c)"), s_wup[nxt % FC], writes=[('wup', nxt % 3)])
                        wc += 1
                        bg = (fc % 2) * 3
                        bv = bg + 1
                        bh = bg + 2
                        for kc in range(KC):
                            mm(ps[:, bg, :], wup[wi][:, kc, 0:128], xh[xi][:, kc, 1:513], kc == 0, kc == KC - 1,
                               reads=[('wup', wi), ('xh', xi)], writes=[B(bg)])
                        for kc in range(KC):
                            mm(ps[:, bh, 0:2], wup[wi][:, kc, 0:128], xh[xi][:, kc, 0:514:513], kc == 0, kc == KC - 1,
                               reads=[('wup', wi), ('xh', xi)], writes=[B(bh)])
                        for kc in range(KC):
                            mm(ps[:, bv, :], wup[wi][:, kc, 128:256], xh[xi][:, kc, 1:513], kc == 0, kc == KC - 1,
                               reads=[('wup', wi), ('xh', xi)], writes=[B(bv)])
                        ai = fc % 2
                        tk.op('dve', lambda e, ai=ai, bg=bg, fc=fc: e.tensor_scalar(out=ab[ai][:], in0=ps[:, bg, :], scalar1=cw[:, fc, 1:2], scalar2=cw[:, fc, 3:4], op0=ALU.mult, op1=ALU.add),
                              reads=[B(bg), 'cw'], writes=[('ab', ai)])
                        tk.op('dve', lambda e, ai=ai, bg=bg, fc=fc: e.scalar_tensor_tensor(out=ab[ai][:, 1:512], in0=ps[:, bg, 0:511], scalar=cw[:, fc, 0:1], in1=ab[ai][:, 1:512], op0=ALU.mult, op1=ALU.add),
                              reads=[B(bg), 'cw', ('ab', ai)], writes=[('ab', ai)])
                        tk.op('dve', lambda e, ai=ai, bg=bg, fc=fc: e.scalar_tensor_tensor(out=ab[ai][:, 0:511], in0=ps[:, bg, 1:512], scalar=cw[:, fc, 2:3], in1=ab[ai][:, 0:511], op0=ALU.mult, op1=ALU.add),
                              reads=[B(bg), 'cw', ('ab', ai)], writes=[('ab', ai)])
                        tk.op('act', lambda e, ai=ai, bh=bh: e.activation(out=hl[ai][:], in_=ps[:, bh, 0:2], func=AF.Copy),
                              reads=[B(bh)], writes=[('hl', ai)])
                        tk.op('act', lambda e, ai=ai, bv=bv: e.activation(out=vb[ai][:], in_=ps[:, bv, :], func=AF.Copy),
                              reads=[B(bv)], writes=[('vb', ai)])
                        tk.op('pool', lambda e, ai=ai, fc=fc: e.tensor_tensor(out=hl[ai][:], in0=hl[ai][:], in1=cw[:, fc, 0:3:2], op=ALU.mult),
                              reads=[('hl', ai), 'cw'], writes=[('hl', ai)])
                        tk.op('pool', lambda e, ai=ai: e.tensor_tensor(out=ab[ai][:, 0:512:511], in0=ab[ai][:, 0:512:511], in1=hl[ai][:], op=ALU.add),
                              reads=[('hl', ai), ('ab', ai)], writes=[('ab', ai)])
                        tk.op('act', lambda e, ai=ai: e.activation(out=ge[ai][:], in_=ab[ai][:], func=AF.Gelu), reads=[('ab', ai)], writes=[('ge', ai)])
                        tk.op('pool', lambda e, ai=ai, fc=fc: e.tensor_tensor(out=ht[:, fc, :], in0=ge[ai][:], in1=vb[ai][:], op=ALU.mult),
                              reads=[('vb', ai), ('ge', ai)], writes=[('ht', fc)])
                    for tt in range(4):
                        ri = tt % 2
                        tk.dma('sp', x2r[ri][:], s_x2[t0 + tt * 128:t0 + (tt + 1) * 128, :], writes=[('x2r', ri)])
                        for nb in range(2):
                            for fc in range(FC):
                                mm(ps[:, 6 + nb, :], ht[:, fc, tt * 128:(tt + 1) * 128], wdn[:, fc, nb * 512:(nb + 1) * 512], fc == 0, fc == FC - 1,
                                   reads=[('wdn', fc), ('ht', fc)], writes=[B(6 + nb)])
                        ya = ps[:, 6:8, :].rearrange("p a b -> p (a b)")
                        tk.op('dve', lambda e, ri=ri, ya=ya: e.scalar_tensor_tensor(out=y3[ri][:], in0=x2r[ri][:], scalar=ALPHA, in1=ya, op0=ALU.mult, op1=ALU.add),
                              reads=[('x2r', ri), B(6), B(7)], writes=[('y3', ri)])
                        layer_norm(lnb, y3[ri][:], ('y3', ri), lnv3[:, 0, :], lnv3[:, 1, :], yo[ri][:], ('yo', ri), 2)
                        tk.dma('sp', y_d[s, t0 + tt * 128:t0 + (tt + 1) * 128, :], yo[ri][:], reads=[('yo', ri)])
                tk.barrier()
        tk.barrier()
    return nc


def _bf16_round(a):
    return a


def host_constants(T):
    pos = np.arange(T, dtype=np.float64)
    p = np.arange(128)
    d = p % 64
    inv = THETA ** (-(np.arange(0, 64, 2, dtype=np.float64)) / 64.0)
    f = d % 32
    ang = pos[None, :] * inv[f][:, None]
    ca = np.cos(ang)
    sa = np.sin(ang) * np.where(d < 32, -1.0, 1.0)[:, None]
    inv16 = THETA ** (-(np.arange(0, 32, 2, dtype=np.float64)) / 32.0)
    dd = d % 32
    fb = dd % 16
    row = np.floor(pos / 64.0)
    col = pos % 64.0
    pp = np.where((d < 32)[:, None], row[None, :], col[None, :])
    angb = pp * inv16[fb][:, None]
    cb = np.cos(angb)
    sbb = np.sin(angb) * np.where(dd < 16, -1.0, 1.0)[:, None]
    rope = np.stack([ca, sa, cb, sbb], axis=1).astype(np.float32)
    i = np.arange(128)[:, None, None]
    j = np.arange(NMASK)[None, :, None]
    q = np.arange(512)[None, None, :]
    dl = -1024 + 128 * j + i - q
    ad = np.abs(dl)
    mult = (ad <= 64).astype(np.float32) + ((dl % 4 == 0) & (ad <= 256)) + ((dl % 16 == 0) & (ad <= 1024))
    mask = mult.astype(np.float32)
    cst = np.zeros((128, 3, 128), np.float32)
    cst[:, 0, :] = np.eye(128)
    cst[0:64, 1, 0:64] = 1.0
    cst[64:128, 1, 64:128] = 1.0
    cst[:, 2, :] = 1.0
    sel = np.zeros((16, KC, 128), np.float32)
    for h in range(16):
        sel[h, h // 2, (h % 2) * 64:(h % 2) * 64 + 64] = 1.0
    return rope, mask, cst, sel.reshape(16, KC * 128)


PERM_A = np.concatenate([np.arange(32, 64), np.arange(0, 32)])
PERM_B = np.concatenate([np.arange(16, 32), np.arange(0, 16), np.arange(48, 64), np.arange(32, 48)])


def _pk(w):
    k, n = w.shape
    return np.ascontiguousarray(w.reshape(k // 128, 128, n).transpose(1, 0, 2))


def _wup_blocked(w):
    g = w[:, :DFF].reshape(KC, 128, FC, 128)
    v = w[:, DFF:].reshape(KC, 128, FC, 128)
    gv = np.concatenate([g, v], axis=3)
    return np.ascontiguousarray(gv.transpose(2, 1, 0, 3).reshape(FC, 128, KC * 256))


def host_weights(w_in, q_norm_g, k_norm_g, out_norm_a, out_norm_b, w_o, ln1_g, ln1_b, wc_q, wc_k, wc_v, wc_o,
                 ln2_g, ln2_b, w_up, conv_w, conv_b, w_down, ln3_g, ln3_b):
    w = w_in[0]
    qa, ka, va, qb, kb, vb = w[:, 0:512], w[:, 512:1024], w[:, 1024:1536], w[:, 1536:2048], w[:, 2048:2176], w[:, 2176:2304]

    def swp(blk, perm):
        n = blk.shape[1] // 64
        return blk.reshape(D, n, 64)[:, :, perm].reshape(D, n * 64)

    kb0, kb1 = kb[:, 0:64], kb[:, 64:128]
    kb0s, kb1s = kb0[:, PERM_B], kb1[:, PERM_B]
    cat = np.concatenate([qa, swp(qa, PERM_A), ka, swp(ka, PERM_A), va, qb, swp(qb, PERM_B),
                          kb0, kb0, kb1, kb1, kb0s, kb0s, kb1s, kb1s, vb], axis=1)
    assert cat.shape[1] == NBLK * 128
    lnv = np.stack([ln1_g[0], ln1_b[0], ln2_g[0], ln2_b[0], ln3_g[0], ln3_b[0]], axis=0)
    lnv = np.ascontiguousarray(np.broadcast_to(lnv[None], (128, 6, D))).astype(np.float32)
    cwv = np.concatenate([conv_w[0], conv_b], axis=0)
    cwv = np.ascontiguousarray(cwv.reshape(4, FC, 128).transpose(2, 1, 0)).astype(np.float32)
    gov = np.concatenate([out_norm_a[0], out_norm_b[0]])
    gov = np.ascontiguousarray(gov.reshape(KC, 128).T).astype(np.float32)
    dd = np.arange(128) % 64
    gqv = np.stack([q_norm_g[0][dd], q_norm_g[0][PERM_B[dd]], k_norm_g[0][dd], k_norm_g[0][PERM_B[dd]]], axis=1).astype(np.float32)
    return dict(wcat=_pk(cat), wo=_pk(w_o[0]), wcq=_pk(wc_q[0]), wck=_pk(wc_k[0]), wcv=_pk(wc_v[0]), wco=_pk(wc_o[0]),
                wup=_wup_blocked(w_up[0]), wdn=_pk(w_down[0]), lnv=lnv, cw=cwv, go=gov, gq=gqv)


def run_sequences(xs, mems, wts, T, NS, n_cores):
    rope, mask, cst, sel = host_constants(T)
    QT = T // 2
    QX = QT + 128
    nseq = len(xs)
    tasks = [(q, h) for q in range(nseq) for h in (0, 1)]
    assert len(tasks) <= NS * n_cores
    wts = dict(wts)
    cw_nat = wts.pop('cw')
    cw_mir = np.ascontiguousarray(cw_nat[:, :, [2, 1, 0, 3]])
    nat = np.arange(T)
    mir = nat[::-1].copy()
    in_maps = []
    for c in range(n_cores):
        xT = np.zeros((NS, D, T), np.float32)
        xx = np.zeros((NS, QX, D), np.float32)
        mT = np.zeros((NS, D, NMEM), np.float32)
        rp = np.zeros((NS, 128, 4, T), np.float32)
        cw = np.zeros((NS, 128, FC, 4), np.float32)
        for s in range(NS):
            idx = s * n_cores + c
            if idx >= len(tasks):
                idx = 0
            q, h = tasks[idx]
            order = nat if h == 0 else mir
            xb = xs[q][order]
            xx[s] = xb[:QX]
            xT[s] = xb.T
            mT[s] = mems[q].T
            rp[s] = rope[:, :, order]
            cw[s] = cw_nat if h == 0 else cw_mir
        m = dict(xT=xT, x=xx, memT=mT, rope=rp, cw=cw, mask=mask, cst=cst, sel=sel)
        m.update(wts)
        in_maps.append(m)
    nc = build_program(T, NS, QT)
    res = run_bass_kernel_spmd(nc, in_maps, core_ids=list(range(n_cores)))
    outs = [np.empty((T, D), np.float32) for _ in range(nseq)]
    for idx, (q, h) in enumerate(tasks):
        s, c = divmod(idx, n_cores)
        yv = np.asarray(res.results[c]["y"][s], dtype=np.float32)
        order = nat if h == 0 else mir
        outs[q][order[:QT]] = yv
    return outs


def kernel(x_prompt, x_sample, mem_prompt, mem_sample, w_in, q_norm_g, k_norm_g, out_norm_a, out_norm_b,
           w_o, ln1_g, ln1_b, wc_q, wc_k, wc_v, wc_o, ln2_g, ln2_b, w_up, conv_w, conv_b, w_down, ln3_g, ln3_b):
    f = lambda a: np.asarray(a, dtype=np.float32)
    wts = host_weights(f(w_in), f(q_norm_g), f(k_norm_g), f(out_norm_a), f(out_norm_b), f(w_o), f(ln1_g), f(ln1_b),
                       f(wc_q), f(wc_k), f(wc_v), f(wc_o), f(ln2_g), f(ln2_b), f(w_up), f(conv_w), f(conv_b),
                       f(w_down), f(ln3_g), f(ln3_b))
    xp, xs_, mp, ms = f(x_prompt), f(x_sample), f(mem_prompt), f(mem_sample)
    seqs = [xp[i] for i in range(xp.shape[0])] + [xs_[i] for i in range(xs_.shape[0])]
    mems = [mp[i] for i in range(mp.shape[0])] + [ms[i] for i in range(ms.shape[0])]
    T = seqs[0].shape[0]
    outs = run_sequences(seqs, mems, wts, T, NS=3, n_cores=8)
    nb = xp.shape[0]
    return (np.stack(outs[:nb], axis=0), np.stack(outs[nb:], axis=0))
```

```python
from contextlib import ExitStack

import numpy as np
import concourse.bass as bass
import concourse.mybir as mybir
from concourse.bass_utils import run_bass_kernel_spmd

F32 = mybir.dt.float32
BF16 = mybir.dt.bfloat16
AF = mybir.ActivationFunctionType
ALU = mybir.AluOpType

D = 1024
KC = 8
DFF = 2816
FC = 22
NMEM = 256
NBLK = 33
ALPHA = float(2.0 ** 0.25)
LN_EPS = 1e-5
RMS_EPS = 1e-6
THETA = 10000.0
QA, QAS, KA, KAS, VA, QB, QBS, KB0, KB1, KB0S, KB1S, VB = 0, 4, 8, 12, 16, 20, 24, 28, 29, 30, 31, 32
NMASK = 20


class Trk:
    def __init__(self, nc, es):
        self.nc = nc
        self.E = {'pe': nc.tensor, 'act': nc.scalar, 'dve': nc.vector, 'pool': nc.gpsimd, 'sp': nc.sync}
        self.sem = {k: es.enter_context(nc.semaphore('s_' + k)) for k in self.E}
        self.cnt = {k: 0 for k in self.E}
        self.seen = {k: {} for k in self.E}
        self.lastw = {}
        self.rd = {}
        self.dpool = {}
        for q, n in (('sp', 24), ('pool', 8), ('act', 8)):
            self.dpool[q] = dict(sems=[es.enter_context(nc.semaphore(f'd_{q}{i}')) for i in range(n)],
                                 val=[0] * n, nxt=0)

    def _semof(self, key):
        return self.sem[key[1]] if key[0] == 'e' else self.dpool[key[1]]['sems'][key[2]]

    def _wait(self, e, key, val):
        if key == ('e', 'pe') and e == 'pe':
            return
        if self.seen[e].get(key, 0) >= val:
            return
        self.E[e].wait_ge(self._semof(key), val)
        self.seen[e][key] = val

    def _deps(self, e, reads, writes):
        for r in reads:
            w = self.lastw.get(r)
            if w:
                self._wait(e, *w)
        for x in writes:
            w = self.lastw.get(x)
            if w:
                self._wait(e, *w)
            for k, v in self.rd.get(x, {}).items():
                self._wait(e, k, v)

    def _record(self, tok, reads, writes):
        for r in reads:
            d = self.rd.setdefault(r, {})
            d[tok[0]] = max(d.get(tok[0], 0), tok[1])
        for x in writes:
            self.lastw[x] = tok
            self.rd[x] = {}

    def op(self, e, fn, reads=(), writes=()):
        self._deps(e, reads, writes)
        ins = fn(self.E[e])
        self.cnt[e] += 1
        ins.then_inc(self.sem[e], 1)
        self._record((('e', e), self.cnt[e]), reads, writes)

    def dma(self, q, out, in_, reads=(), writes=()):
        dp = self.dpool[q]
        i = dp['nxt']
        dp['nxt'] = (i + 1) % len(dp['sems'])
        if dp['val'][i] > 0:
            self._wait(q, ('d', q, i), dp['val'][i])
        self._deps(q, reads, writes)
        ins = self.E[q].dma_start(out=out, in_=in_)
        dp['val'][i] += 16
        ins.then_inc(dp['sems'][i], 16)
        self._record((('d', q, i), dp['val'][i]), reads, writes)

    def barrier(self):
        for e in self.E:
            for e2 in self.E:
                if e2 != e and self.cnt[e2] > 0:
                    self._wait(e, ('e', e2), self.cnt[e2])
            for q, dp in self.dpool.items():
                for i, v in enumerate(dp['val']):
                    if v > 0:
                        self._wait(e, ('d', q, i), v)
        self.lastw = {}
        self.rd = {}


def build_program(T, NS, QT=None):
    NPC = T // 512
    NT = T // 128
    QT = T if QT is None else QT
    HALO = 128 if QT < T else 0
    QPC = QT // 512
    qpieces = [(pc * 512, 512) for pc in range(QPC)] + ([(QT, 128)] if HALO else [])
    QX = QT + HALO
    nc = bass.Bass("TRN2", target_bir_lowering=False)

    def din(name, shape, dt=F32):
        return nc.dram_tensor(name, list(shape), dt, kind="ExternalInput").ap()

    xT_d = din("xT", [NS, D, T])
    x_d = din("x", [NS, QX, D])
    memT_d = din("memT", [NS, D, NMEM])
    wcat_d = din("wcat", [128, KC, NBLK * 128])
    wo_d = din("wo", [128, KC, D])
    wcq_d = din("wcq", [128, KC, D])
    wck_d = din("wck", [128, KC, D])
    wcv_d = din("wcv", [128, KC, D])
    wco_d = din("wco", [128, KC, D])
    wup_d = din("wup", [FC, 128, KC * 256])
    wdn_d = din("wdn", [128, FC, D])
    rope_d = din("rope", [NS, 128, 4, T])
    mask_d = din("mask", [128, NMASK, 512])
    lnv_d = din("lnv", [128, 6, D])
    cw_d = din("cw", [NS, 128, FC, 4])
    go_d = din("go", [128, KC])
    gq_d = din("gq", [128, 4])
    cst_d = din("cst", [128, 3, 128])
    sel_d = din("sel", [16, KC * 128])
    y_d = nc.dram_tensor("y", [NS, QT, D], F32, kind="ExternalOutput").ap()

    s_qa = nc.dram_tensor("s_qa", [4, 128, T], BF16).ap()
    s_ka = nc.dram_tensor("s_ka", [4, 128, T], BF16).ap()
    s_va = nc.dram_tensor("s_va", [4, 128, T], BF16).ap()
    s_qb = nc.dram_tensor("s_qb", [4, 128, T], BF16).ap()
    s_kb = nc.dram_tensor("s_kb", [2, 128, T], BF16).ap()
    s_vb = nc.dram_tensor("s_vb", [128, T], BF16).ap()
    s_mix = nc.dram_tensor("s_mix", [D, T], BF16).ap()
    s_x2t = nc.dram_tensor("s_x2t", [D, T], BF16).ap()
    s_x2 = nc.dram_tensor("s_x2", [T, D], F32).ap()
    s_wup = nc.dram_tensor("s_wup", [FC, 128, KC * 256], BF16).ap()
    s_den = nc.dram_tensor("s_den", [16, T], F32).ap()

    with ExitStack() as es:
        tk = Trk(nc, es)

        uid = [0]

        def sb(stack, name, shape, dt):
            uid[0] += 1
            return stack.enter_context(nc.sbuf_tensor(f"sb{uid[0]}_{name}", list(shape), dt))

        ps = es.enter_context(nc.psum_tensor("psum_all", [128, 8, 512], F32))

        def B(i):
            return ('ps', i)

        cstb = sb(es, "cstb", [128, 3, 128], BF16)
        cstf = sb(es, "cstf", [128, 512], F32)
        go = sb(es, "go", [128, KC], F32)
        gq = sb(es, "gq", [128, 4], F32)
        wo = sb(es, "wo", [128, KC, D], BF16)
        epsc = sb(es, "epsc", [128, 4], F32)
        tk.dma('pool', cstb[:], cst_d, writes=['cstb'])
        tk.dma('sp', go[:], go_d, writes=['go'])
        tk.dma('sp', gq[:], gq_d, writes=['gq'])
        tk.op('dve', lambda e: e.memset(cstf[:], 1.0), writes=['cstf'])
        tk.op('dve', lambda e: e.memset(epsc[:, 0:1], LN_EPS), writes=['epsc0'])
        tk.op('dve', lambda e: e.memset(epsc[:, 1:2], RMS_EPS), writes=['epsc1'])
        tk.op('dve', lambda e: e.memset(epsc[:, 2:3], 64.0 * RMS_EPS), writes=['epsc2'])
        tk.op('dve', lambda e: e.memset(epsc[:, 3:4], 0.0), writes=['epsc3'])
        IDB = cstb[:, 0, :]
        BDB = cstb[:, 1, :]
        ONB = cstb[:, 2, :]
        with ExitStack() as st0:
            wof = sb(st0, "wof", [128, KC, D], F32)
            tk.dma('sp', wof[:], wo_d, writes=['wof'])
            for kc in range(KC):
                tk.op('dve', lambda e, kc=kc: e.tensor_scalar(out=wo[:, kc, :], in0=wof[:, kc, :],
                                                              scalar1=go[:, kc:kc + 1], scalar2=None, op0=ALU.mult),
                      reads=['wof', 'go'], writes=[('wo', kc)])
            wst = [sb(st0, f"wst{i}", [128, KC * 256], BF16) for i in range(2)]
            for fc in range(FC):
                tk.dma('pool', wst[fc % 2][:], wup_d[fc], writes=[('wst', fc % 2)])
                tk.dma('sp', s_wup[fc], wst[fc % 2][:], reads=[('wst', fc % 2)])
            tk.barrier()

        def mm(out, lhsT, rhs, start, stop, reads, writes):
            tk.op('pe', lambda e: e.matmul(out, lhsT, rhs, start=start, stop=stop), reads=reads, writes=writes)

        def layer_norm(stack_bufs, src_ap, src_key, g_ap, b_ap, out_ap, out_key, tagi):
            stt, mv, rstd = stack_bufs
            k = ('ln', tagi)
            tk.op('dve', lambda e: e.bn_stats(stt[:, 0:6], src_ap[:, 0:512]), reads=[src_key], writes=[(k, 's0')])
            tk.op('dve', lambda e: e.bn_stats(stt[:, 6:12], src_ap[:, 512:1024]), reads=[src_key], writes=[(k, 's1')])
            tk.op('dve', lambda e: e.bn_aggr(mv[:], stt[:]), reads=[(k, 's0'), (k, 's1')], writes=[(k, 'mv')])
            tk.op('act', lambda e: e.activation(out=rstd[:, 0:1], in_=mv[:, 1:2], func=AF.Sqrt, bias=epsc[:, 0:1], scale=1.0),
                  reads=[(k, 'mv'), 'epsc0'], writes=[(k, 'sd')])
            tk.op('dve', lambda e: e.reciprocal(out=rstd[:, 1:2], in_=rstd[:, 0:1]), reads=[(k, 'sd')], writes=[(k, 'rs')])
            tk.op('dve', lambda e: e.scalar_tensor_tensor(out=src_ap, in0=src_ap, scalar=mv[:, 0:1], in1=g_ap, op0=ALU.subtract, op1=ALU.mult),
                  reads=[src_key, (k, 'mv'), 'lnv'], writes=[src_key])
            tk.op('dve', lambda e: e.scalar_tensor_tensor(out=out_ap, in0=src_ap, scalar=rstd[:, 1:2], in1=b_ap, op0=ALU.mult, op1=ALU.add),
                  reads=[src_key, (k, 'rs'), 'lnv'], writes=[out_key])

        for s in range(NS):
            xTv = xT_d[s].rearrange("(kc p) t -> p kc t", p=128)
            with ExitStack() as st:
                wcat = sb(st, "wcat", [128, KC, NBLK * 128], BF16)
                xt = [sb(st, f"xt{i}", [128, KC, 512], BF16) for i in range(2)]
                tb = [sb(st, f"tb{i}", [128, 4, 512], F32) for i in range(2)]
                gt = sb(st, "gt", [128, 4, 512], F32)
                t1 = [sb(st, f"t1_{i}", [128, 512], F32) for i in range(2)]
                t2 = [sb(st, f"t2_{i}", [128, 512], F32) for i in range(2)]
                rs = [sb(st, f"rs_{i}", [128, 512], F32) for i in range(2)]
                sq = [sb(st, f"sq_{i}", [128, 512], BF16) for i in range(2)]
                og = [sb(st, f"og_{i}", [128, 512], BF16) for i in range(4)]
                for kc in range(KC):
                    pass
                WCH = [(0, 8), (8, 16), (16, 24), (24, 33)]
                for ci, (b_lo, b_hi) in enumerate(WCH):
                    tk.dma('pool', wcat[:, :, b_lo * 128:b_hi * 128], wcat_d[:, :, b_lo * 128:b_hi * 128], writes=[('wcat', ci)])

                def wcat_key(blk):
                    for ci, (b_lo, b_hi) in enumerate(WCH):
                        if b_lo <= blk < b_hi:
                            return ('wcat', ci)
                cnt = dict(bank=0, t=0, og=0)

                def nbank():
                    b = cnt['bank']
                    cnt['bank'] = (b + 1) % 8
                    return b

                def proj(blk, xi, bank):
                    for kc in range(KC):
                        mm(ps[:, bank, :], wcat[:, kc, blk * 128:(blk + 1) * 128], xt[xi][:, kc, :],
                           kc == 0, kc == KC - 1, reads=[('xt', xi)] + ([wcat_key(blk)] if kc == 0 else []), writes=[B(bank)])

                def store(oi, dst):
                    tk.dma('sp', dst, og[oi][:], reads=[('og', oi)])

                def next_og():
                    o = cnt['og']
                    cnt['og'] = (o + 1) % 4
                    return o

                def load_piece(pc):
                    i = pc % 2
                    t0 = pc * 512
                    tk.dma('pool', xt[i][:], xTv[:, :, t0:t0 + 512], writes=[('xt', i)])
                    tk.dma('sp', tb[i][:], rope_d[s, :, :, t0:t0 + 512], writes=[('tb', i)])

                load_piece(0)
                for pc in range(NPC):
                    i = pc % 2
                    t0 = pc * 512
                    if pc + 1 < NPC:
                        load_piece(pc + 1)
                    for j, (tbi, gcol) in enumerate(((2, 0), (3, 1), (2, 2), (3, 3))):
                        tk.op('pool', lambda e, j=j, tbi=tbi, gcol=gcol: e.tensor_scalar(
                            out=gt[:, j, :], in0=tb[i][:, tbi, :], scalar1=gq[:, gcol:gcol + 1], scalar2=8.0,
                            op0=ALU.mult, op1=ALU.mult), reads=[('tb', i), 'gq'], writes=[('gt', j)])
                    need_q = t0 < QX
                    need_ka = t0 < QT + 1536
                    for (b0, bs, dst) in ([(QA, QAS, s_qa)] if need_q else []) + ([(KA, KAS, s_ka)] if need_ka else []):
                        for j in range(4):
                            bx, by = nbank(), nbank()
                            proj(b0 + j, i, bx)
                            proj(bs + j, i, by)
                            ti = cnt['t']
                            cnt['t'] = (ti + 1) % 2
                            tk.op('dve', lambda e, bx=bx, ti=ti: e.tensor_tensor(out=t1[ti][:], in0=ps[:, bx, :], in1=tb[i][:, 0, :], op=ALU.mult),
                                  reads=[B(bx), ('tb', i)], writes=[('t1', ti)])
                            tk.op('dve', lambda e, by=by, ti=ti: e.tensor_tensor(out=t2[ti][:], in0=ps[:, by, :], in1=tb[i][:, 1, :], op=ALU.mult),
                                  reads=[B(by), ('tb', i)], writes=[('t2', ti)])
                            oi = next_og()
                            tk.op('pool', lambda e, ti=ti, oi=oi: e.tensor_tensor(out=og[oi][:], in0=t1[ti][:], in1=t2[ti][:], op=ALU.add),
                                  reads=[('t1', ti), ('t2', ti)], writes=[('og', oi)])
                            store(oi, dst[j, :, t0:t0 + 512])
                    for (blk, dst) in ([(VA + j, s_va[j, :, t0:t0 + 512]) for j in range(4)] if need_ka else []) + [(VB, s_vb[:, t0:t0 + 512])]:
                        bx = nbank()
                        proj(blk, i, bx)
                        oi = next_og()
                        tk.op('act', lambda e, bx=bx, oi=oi: e.activation(out=og[oi][:], in_=ps[:, bx, :], func=AF.Copy),
                              reads=[B(bx)], writes=[('og', oi)])
                        store(oi, dst)
                    pairs = ([(QB + j, QBS + j, 0, s_qb[j, :, t0:t0 + 512]) for j in range(4)] if need_q else []) + \
                            [(KB0, KB0S, 2, s_kb[0, :, t0:t0 + 512]), (KB1, KB1S, 2, s_kb[1, :, t0:t0 + 512])]
                    for (bq, bqs, gj, dst) in pairs:
                        bx, by, bz = nbank(), nbank(), nbank()
                        proj(bq, i, bx)
                        proj(bqs, i, by)
                        ti = cnt['t']
                        cnt['t'] = (ti + 1) % 2
                        tk.op('act', lambda e, bx=bx, ti=ti: e.activation(out=sq[ti][:], in_=ps[:, bx, :], func=AF.Square),
                              reads=[B(bx)], writes=[('sq', ti)])
                        mm(ps[:, bz, :], BDB, sq[ti][:], True, True, reads=[('sq', ti), 'cstb'], writes=[B(bz)])
                        tk.op('act', lambda e, bz=bz, ti=ti: e.activation(out=t2[ti][:], in_=ps[:, bz, :], func=AF.Sqrt, bias=epsc[:, 2:3], scale=1.0),
                              reads=[B(bz), 'epsc2'], writes=[('t2', ti)])
                        tk.op('dve', lambda e, ti=ti: e.reciprocal(out=rs[ti][:], in_=t2[ti][:]), reads=[('t2', ti)], writes=[('rs', ti)])
                        tk.op('dve', lambda e, bx=bx, ti=ti, gj=gj: e.tensor_tensor(out=t1[ti][:], in0=ps[:, bx, :], in1=gt[:, gj, :], op=ALU.mult),
                              reads=[B(bx), ('gt', gj)], writes=[('t1', ti)])
                        tk.op('dve', lambda e, by=by, ti=ti, gj=gj: e.tensor_tensor(out=t2[ti][:], in0=ps[:, by, :], in1=gt[:, gj + 1, :], op=ALU.mult),
                              reads=[B(by), ('gt', gj + 1)], writes=[('t2', ti)])
                        tk.op('pool', lambda e, ti=ti: e.tensor_tensor(out=t1[ti][:], in0=t1[ti][:], in1=t2[ti][:], op=ALU.add),
                              reads=[('t1', ti), ('t2', ti)], writes=[('t1', ti)])
                        oi = next_og()
                        tk.op('pool', lambda e, ti=ti, oi=oi: e.tensor_tensor(out=og[oi][:], in0=t1[ti][:], in1=rs[ti][:], op=ALU.mult),
                              reads=[('t1', ti), ('rs', ti)], writes=[('og', oi)])
                        store(oi, dst)
                tk.barrier()

            def attention(st, tag, jobs, use_mask, mk):
                qzs = [[sb(st, tag + f"qz{jb}_{i}", [128, T], BF16) for i in range(2)] for jb in range(2)]
                ksbs = [sb(st, tag + f"k{jb}", [128, T], BF16) for jb in range(2)]
                NTU = NT if not use_mask else min(NT, max((q0 - 1024) // 128 + NMASK for (q0, _) in qpieces))
                vsbs = [sb(st, tag + f"v{jb}", [128, NTU * 128], BF16) for jb in range(2 if use_mask else 1)]
                vtm = sb(st, tag + "vtm", [128, NT, 2, 128], BF16)
                pt = [sb(st, tag + f"pt{i}", [128, 3, 512], BF16) for i in range(4)]
                rden = [sb(st, tag + f"rd{i}", [128, 512], F32) for i in range(2)]
                ost = [sb(st, tag + f"os{i}", [128, 512], BF16) for i in range(2)]
                tk.op('dve', lambda e: e.memset(vtm[:, :, :, 64:128], 1.0), writes=['vtm1'])
                for jb in range(2):
                    tk.op('pool', lambda e, jb=jb: e.memset(qzs[jb][0][64:128, :], 0.0), writes=[('qzpad', jb, 0)])
                    tk.op('pool', lambda e, jb=jb: e.memset(qzs[jb][1][0:64, :], 0.0), writes=[('qzpad', jb, 1)])

                def load_job(ji):
                    jb = ji % 2
                    q_src, k_src = jobs[ji][0], jobs[ji][1]
                    tk.dma('sp', qzs[jb][0][0:64, :], q_src[0:64, :], writes=[('qz', jb, 0)])
                    tk.dma('sp', qzs[jb][1][64:128, :], q_src[64:128, :], writes=[('qz', jb, 64)])
                    tk.dma('sp', ksbs[jb][:], k_src, writes=[('ksb', jb)])
                    if ji == 0 or jobs[ji][2] is not jobs[ji - 1][2]:
                        tk.dma('sp', vsbs[jb % len(vsbs)][:, 0:NTU * 128], jobs[ji][2][:, 0:NTU * 128], writes=[('vsb', jb % len(vsbs))])

                load_job(0)
                c = dict(g=0, p=0, o=0, m=0)
                last = dict(k=None, v=None)
                for ji, (q_src, k_src, v_src, heads) in enumerate(jobs):
                    jb = ji % 2
                    qz = qzs[jb]
                    ksb = ksbs[jb]
                    if ji + 1 < len(jobs):
                        load_job(ji + 1)
                    if ji == 0 or jobs[ji][2] is not jobs[ji - 1][2]:
                        vsb = vsbs[jb % len(vsbs)]
                        for t4 in range(0, NTU, 4):
                            bk = 6 + (t4 // 4) % 2
                            pb = ps[:, bk, :].bitcast(BF16)
                            n4 = min(4, NTU - t4)
                            for u in range(n4):
                                tk.op('pe', lambda e, u=u, t4=t4, pb=pb: e.transpose(pb[:, u * 128:(u + 1) * 128], vsb[:, (t4 + u) * 128:(t4 + u + 1) * 128], IDB),
                                      reads=[('vsb', jb % len(vsbs)), 'cstb'], writes=[B(bk)])
                            tk.op('act', lambda e, t4=t4, n4=n4, pb=pb: e.activation(
                                out=vtm[:, t4:t4 + n4, :, 0:64],
                                in_=pb[:, 0:n4 * 128].rearrange("p (a h e) -> p a h e", h=2, e=64), func=AF.Copy),
                                reads=[B(bk)], writes=['vtm'])
                    tasks = []
                    for (q0, qw) in qpieces:
                        if use_mask:
                            tiles = [(j, (q0 - 1024) // 128 + j) for j in range(NMASK)]
                            tiles = [(j, kt) for (j, kt) in tiles if 0 <= kt < NT]
                        else:
                            tiles = [(None, kt) for kt in range(NT)]
                        groups = [tiles[a:a + 3] for a in range(0, len(tiles), 3)]
                        for (r0, vh, mrow) in heads:
                            for gi, grp in enumerate(groups):
                                tasks.append(dict(q0=q0, w=qw, r0=r0, vh=vh, mrow=mrow, grp=grp, first=(gi == 0), last=(gi == len(groups) - 1)))

                    def emit_s(t):
                        g3 = (c['g'] % 2) * 3
                        c['g'] += 1
                        t['g3'] = g3
                        r0, q0, w = t['r0'], t['q0'], t['w']
                        for u, (j, kt) in enumerate(t['grp']):
                            mm(ps[:, g3 + u, 0:w], ksb[:, kt * 128:(kt + 1) * 128], qz[r0 // 64][:, q0:q0 + w],
                               True, True, reads=[('ksb', jb), ('qz', jb, r0), ('qzpad', jb, r0 // 64)], writes=[B(g3 + u)])

                    def emit_exp(t):
                        g3 = t['g3']
                        grp = t['grp']
                        n = len(grp)
                        pi = c['p'] % 4
                        c['p'] += 1
                        t['pi'] = pi
                        w = t['w']
                        tk.op('act', lambda e: e.activation(out=pt[pi][:, 0:n, 0:w], in_=ps[:, g3:g3 + n, 0:w], func=AF.Exp, scale=0.125),
                              reads=[B(g3 + u) for u in range(n)], writes=[('pt', pi)])
                        if use_mask:
                            j0 = grp[0][0]
                            tk.op('dve', lambda e: e.tensor_tensor(out=pt[pi][:, 0:n, 0:w], in0=pt[pi][:, 0:n, 0:w], in1=mk[:, j0:j0 + n, 0:w], op=ALU.mult),
                                  reads=[('pt', pi), 'mk'], writes=[('pt', pi)])

                    def emit_pv(t):
                        grp = t['grp']
                        n = len(grp)
                        pi = t['pi']
                        if t['first']:
                            c['o'] += 1
                        ob = 6 + c['o'] % 2
                        oi = c['o'] % 2
                        w = t['w']
                        for u, (j, kt) in enumerate(grp):
                            mm(ps[:, ob, 0:w], vtm[:, kt, t['vh'], :], pt[pi][:, u, 0:w], t['first'] and u == 0, t['last'] and u == n - 1,
                               reads=[('pt', pi), 'vtm', 'vtm1'], writes=[B(ob)])
                        return ob, oi

                    def tail(ob, oi, mrow, q0, w):
                        ceng = 'act' if use_mask else 'dve'
                        if ceng == 'act':
                            tk.op('act', lambda e: e.activation(out=ost[oi][0:64, 0:w], in_=ps[0:64, ob, 0:w], func=AF.Copy),
                                  reads=[B(ob)], writes=[('ost', oi)])
                        else:
                            tk.op('dve', lambda e: e.tensor_copy(out=ost[oi][0:64, 0:w], in_=ps[0:64, ob, 0:w]),
                                  reads=[B(ob)], writes=[('ost', oi)])
                        tk.op('dve', lambda e: e.tensor_copy(out=rden[oi][64:65, 0:w], in_=ps[64:65, ob, 0:w]),
                              reads=[B(ob)], writes=[('rden', oi)])
                        tk.dma('sp', s_mix[mrow:mrow + 64, q0:q0 + w], ost[oi][0:64, 0:w], reads=[('ost', oi)])
                        hd = mrow // 64
                        tk.dma('sp', s_den[hd:hd + 1, q0:q0 + w], rden[oi][64:65, 0:w], reads=[('rden', oi)])

                    emit_s(tasks[0])
                    if len(tasks) > 1:
                        emit_s(tasks[1])
                    pend = []
                    for ti, t in enumerate(tasks):
                        emit_exp(t)
                        for pdn in pend:
                            pdn[0] -= 1
                        while pend and pend[0][0] <= 0:
                            tail(*pend.pop(0)[1])
                        if ti + 2 < len(tasks):
                            emit_s(tasks[ti + 2])
                        ob, oi = emit_pv(t)
                        if t['last']:
                            pend.append([2, (ob, oi, t['mrow'], t['q0'], t['w'])])
                    while pend:
                        tail(*pend.pop(0)[1])

            with ExitStack() as st:
                mk = sb(st, "mk", [128, NMASK, 512], BF16)
                tk.dma('pool', mk[:], mask_d, writes=['mk'])
                jobs = []
                for hp in range(4):
                    jobs.append((s_qa[hp], s_ka[hp], s_va[hp],
                                 [(0, 0, hp * 128), (64, 1, hp * 128 + 64)]))
                attention(st, "a", jobs, True, mk)
                tk.barrier()
            with ExitStack() as st:
                jobs = []
                for hp in range(4):
                    g = hp // 2
                    jobs.append((s_qb[hp], s_kb[g], s_vb,
                                 [(0, g, 512 + hp * 128), (64, g, 512 + hp * 128 + 64)]))
                attention(st, "b", jobs, False, None)
                tk.barrier()

            with ExitStack() as st:
                wcq = sb(st, "wcq", [128, KC, D], BF16)
                wco = sb(st, "wco", [128, KC, D], BF16)
                km = sb(st, "km", [128, KC, NMEM], BF16)
                vm = sb(st, "vm", [128, 2, D], BF16)
                lnv = sb(st, "lnv", [128, 4, D], F32)
                mxs = [sb(st, f"mx{i}", [128, KC, 512], BF16) for i in range(2)]
                sqms = [sb(st, f"sqm{i}", [128, KC, 512], BF16) for i in range(2)]
                xtm = [sb(st, f"xtm{i}", [128, D], F32) for i in range(2)]
                yb = sb(st, "yb", [128, D], F32)
                x1 = sb(st, "x1", [128, 4, D], F32)
                xbf = [sb(st, f"xbf{i}", [128, D], BF16) for i in range(2)]
                x1t = sb(st, "x1t", [128, KC, 512], BF16)
                qc = sb(st, "qc", [128, KC, 512], BF16)
                ptc = [sb(st, f"ptc{i}", [128, 2, 512], BF16) for i in range(2)]
                rdc = [sb(st, f"rdc{i}", [128, 512], F32) for i in range(2)]
                oc = sb(st, "oc", [128, KC, 512], BF16)
                x2 = [sb(st, f"x2_{i}", [128, D], F32) for i in range(2)]
                x2ts = [sb(st, f"x2ts{i}", [128, KC, 128], BF16) for i in range(2)]
                rsabs = [sb(st, f"rsab{i}", [128, 16], F32) for i in range(2)]
                lnb = (sb(st, "lnst", [128, 12], F32), sb(st, "lnmv", [128, 2], F32), sb(st, "lnrs", [128, 2], F32))
                st_in = ExitStack()
                wtmp = sb(st_in, "wtmp", [128, KC, D], BF16)
                memt = sb(st_in, "memt", [128, KC, NMEM], BF16)
                tk.dma('pool', wcq[:], wcq_d, writes=['wcq'])
                tk.dma('pool', wco[:], wco_d, writes=['wco'])
                tk.dma('sp', lnv[:], lnv_d[:, 0:4, :], writes=['lnv'])
                tk.dma('pool', memt[:], memT_d[s].rearrange("(kc p) m -> p kc m", p=128), writes=['memt'])
                tk.dma('pool', wtmp[:], wck_d, writes=['wtmp'])
                for cb in range(KC):
                    bk = cb % 4
                    for kc in range(KC):
                        mm(ps[:, bk, 0:NMEM], wtmp[:, kc, cb * 128:(cb + 1) * 128], memt[:, kc, :], kc == 0, kc == KC - 1,
                           reads=['wtmp', 'memt'], writes=[B(bk)])
                    tk.op('act', lambda e, bk=bk, cb=cb: e.activation(out=km[:, cb, :], in_=ps[:, bk, 0:NMEM], func=AF.Copy),
                          reads=[B(bk)], writes=['km'])
                tk.dma('pool', wtmp[:], wcv_d, reads=[], writes=['wtmp'])
                for mt in range(2):
                    for nb in range(2):
                        bk = 4 + (mt * 2 + nb) % 4
                        for kc in range(KC):
                            mm(ps[:, bk, :], memt[:, kc, mt * 128:(mt + 1) * 128], wtmp[:, kc, nb * 512:(nb + 1) * 512], kc == 0, kc == KC - 1,
                               reads=['wtmp', 'memt'], writes=[B(bk)])
                        tk.op('act', lambda e, bk=bk, mt=mt, nb=nb: e.activation(out=vm[:, mt, nb * 512:(nb + 1) * 512], in_=ps[:, bk, :], func=AF.Copy),
                              reads=[B(bk)], writes=['vm'])

                tk.barrier()
                st_in.close()
                dn16 = [sb(st, f"dn16_{i}", [16, 512], F32) for i in range(2)]
                rd16 = [sb(st, f"rd16_{i}", [16, 512], F32) for i in range(2)]
                selc = sb(st, "selc", [16, KC * 128], F32)
                tk.dma('sp', selc[:], sel_d, writes=['selc'])
                mixv = s_mix.rearrange("(kc p) t -> p kc t", p=128)
                x2tv = s_x2t.rearrange("(kc p) t -> p kc t", p=128)
                lnc = 0
                def prep_load(qi):
                    t0, W = qpieces[qi]
                    m = qi % 2
                    tk.dma('sp', mxs[m][:, :, 0:W], mixv[:, :, t0:t0 + W], writes=[('mx', m)])
                    tk.dma('sp', dn16[m][:, 0:W], s_den[:, t0:t0 + W], writes=[('dn16', m)])

                def prep_norm(qi):
                    t0, W = qpieces[qi]
                    m = qi % 2
                    tk.op('dve', lambda e: e.reciprocal(out=rd16[m][:, 0:W], in_=dn16[m][:, 0:W]), reads=[('dn16', m)], writes=[('rd16', m)])
                    for kc in range(KC):
                        bk = 6 + kc % 2
                        mm(ps[:, bk, 0:W], selc[:, kc * 128:(kc + 1) * 128], rd16[m][:, 0:W], True, True,
                           reads=[('rd16', m), 'selc'], writes=[B(bk)])
                        tk.op('dve', lambda e, kc=kc, bk=bk: e.tensor_tensor(out=mxs[m][:, kc, 0:W], in0=mxs[m][:, kc, 0:W], in1=ps[:, bk, 0:W], op=ALU.mult),
                              reads=[B(bk), ('mx', m)], writes=[('mx', m)])
                    tk.op('pool', lambda e: e.tensor_tensor(out=sqms[m][:, :, 0:W], in0=mxs[m][:, :, 0:W], in1=mxs[m][:, :, 0:W], op=ALU.mult),
                          reads=[('mx', m)], writes=[('sqm', m)])

                def prep_ssq(qi):
                    t0, W = qpieces[qi]
                    m = qi % 2
                    for tt in range(W // 128):
                        for half in range(2):
                            col = tt * 2 + half
                            for k4 in range(4):
                                kc = half * 4 + k4
                                mm(ps[:, 7, col:col + 1], sqms[m][:, kc, tt * 128:(tt + 1) * 128], ONB[:, 0:1], k4 == 0, k4 == 3,
                                   reads=[('sqm', m), 'cstb'], writes=[B(7)])
                    tk.op('act', lambda e: e.activation(out=rsabs[m][:, 0:8], in_=ps[:, 7, 0:8], func=AF.Sqrt, bias=epsc[:, 1:2], scale=1.0 / 512.0),
                          reads=[B(7), 'epsc1'], writes=[('rsab0', m)])
                    tk.op('dve', lambda e: e.reciprocal(out=rsabs[m][:, 8:16], in_=rsabs[m][:, 0:8]), reads=[('rsab0', m)], writes=[('rsab', m)])

                prep_load(0)
                prep_norm(0)
                prep_ssq(0)
                for qi, (t0, W) in enumerate(qpieces):
                    NTT = W // 128
                    mx = mxs[qi % 2]
                    rsab = rsabs[qi % 2]
                    mkey = ('mx', qi % 2)
                    rkey = ('rsab', qi % 2)
                    if qi + 1 < len(qpieces):
                        prep_load(qi + 1)
                    pend1 = []
                    for tt in range(NTT):
                        xi = tt % 2
                        tk.dma('sp', xtm[xi][:], x_d[s, t0 + tt * 128:t0 + (tt + 1) * 128, :], writes=[('xtm', xi)])
                        for half in range(2):
                            for nb in range(2):
                                bk = half * 2 + nb
                                for k4 in range(4):
                                    kc = half * 4 + k4
                                    mm(ps[:, bk, :], mx[:, kc, tt * 128:(tt + 1) * 128], wo[:, kc, nb * 512:(nb + 1) * 512], k4 == 0, k4 == 3,
                                       reads=[mkey] + [('wo', kc)], writes=[B(bk)])
                        while pend1:
                            pend1.pop(0)()
                        ya = ps[:, 0:2, :].rearrange("p a b -> p (a b)")
                        ybm = ps[:, 2:4, :].rearrange("p a b -> p (a b)")
                        ca = 8 + tt * 2
                        tk.op('dve', lambda e, ya=ya, ca=ca: e.tensor_scalar(out=yb[:], in0=ya, scalar1=rsab[:, ca:ca + 1], scalar2=None, op0=ALU.mult),
                              reads=[B(0), B(1), rkey], writes=['yb'])
                        tk.op('dve', lambda e, ybm=ybm, ca=ca: e.scalar_tensor_tensor(out=yb[:], in0=ybm, scalar=rsab[:, ca + 1:ca + 2], in1=yb[:], op0=ALU.mult, op1=ALU.add),
                              reads=[B(2), B(3), rkey, 'yb'], writes=['yb'])
                        tk.op('dve', lambda e, xi=xi: e.scalar_tensor_tensor(out=yb[:], in0=xtm[xi][:], scalar=ALPHA, in1=yb[:], op0=ALU.mult, op1=ALU.add),
                              reads=[('xtm', xi), 'yb'], writes=['yb'])
                        layer_norm(lnb, yb[:], 'yb', lnv[:, 0, :], lnv[:, 1, :], x1[:, tt, :], ('x1', tt), 0)
                        xb_i = tt % 2
                        tk.op('act', lambda e, tt=tt, xb_i=xb_i: e.activation(out=xbf[xb_i][:], in_=x1[:, tt, :], func=AF.Copy), reads=[('x1', tt)], writes=[('xbf', xb_i)])

                        def tr1(tt=tt, xb_i=xb_i):
                            bkt = 4 + xb_i
                            pb = ps[:, bkt, :].bitcast(BF16)
                            for kc in range(KC):
                                tk.op('pe', lambda e, kc=kc: e.transpose(pb[:, kc * 128:(kc + 1) * 128], xbf[xb_i][:, kc * 128:(kc + 1) * 128], IDB),
                                      reads=[('xbf', xb_i), 'cstb'], writes=[B(bkt)])
                            tk.op('act', lambda e: e.activation(out=x1t[:, :, tt * 128:(tt + 1) * 128],
                                                                in_=pb.rearrange("p (k t) -> p k t", k=KC), func=AF.Copy),
                                  reads=[B(bkt)], writes=['x1t'])
                        pend1.append(tr1)
                    while pend1:
                        pend1.pop(0)()
                    if qi + 1 < len(qpieces):
                        prep_norm(qi + 1)
                    for cb in range(KC):
                        bk = cb % 4
                        for kc in range(KC):
                            mm(ps[:, bk, 0:W], wcq[:, kc, cb * 128:(cb + 1) * 128], x1t[:, kc, 0:W], kc == 0, kc == KC - 1,
                               reads=['wcq', 'x1t'], writes=[B(bk)])
                        tk.op('act', lambda e, bk=bk, cb=cb: e.activation(out=qc[:, cb, 0:W], in_=ps[:, bk, 0:W], func=AF.Copy),
                              reads=[B(bk)], writes=[('qc', cb)])
                    if qi + 1 < len(qpieces):
                        prep_ssq(qi + 1)

                    def ca_x(hc):
                        pi = hc % 2
                        sbk = (4, 5) if pi == 0 else (2, 3)
                        dbk = 6 if pi == 0 else 1
                        for mt in range(2):
                            for j in range(2):
                                mm(ps[:, sbk[mt], 0:W], km[:, 2 * hc + j, mt * 128:(mt + 1) * 128], qc[:, 2 * hc + j, 0:W], j == 0, j == 1,
                                   reads=['km', ('qc', 2 * hc + j)], writes=[B(sbk[mt])])
                        tk.op('act', lambda e: e.activation(out=ptc[pi][:, :, 0:W], in_=ps[:, sbk[0]:sbk[0] + 2, 0:W], func=AF.Exp, scale=1.0 / 16.0),
                              reads=[B(sbk[0]), B(sbk[1])], writes=[('ptc', pi)])
                        for mt in range(2):
                            mm(ps[:, dbk, 0:W], ONB, ptc[pi][:, mt, 0:W], mt == 0, mt == 1, reads=[('ptc', pi), 'cstb'], writes=[B(dbk)])
                        tk.op('dve', lambda e: e.reciprocal(out=rdc[pi][:, 0:W], in_=ps[:, dbk, 0:W]), reads=[B(dbk)], writes=[('rdc', pi)])

                    def ca_y(hc):
                        pi = hc % 2
                        pvb = 7 if pi == 0 else 0
                        for j in range(2):
                            for mt in range(2):
                                mm(ps[:, pvb, 0:W], vm[:, mt, hc * 256 + j * 128:hc * 256 + (j + 1) * 128], ptc[pi][:, mt, 0:W], mt == 0, mt == 1,
                                   reads=[('ptc', pi), 'vm'], writes=[B(pvb)])
                            tk.op('dve', lambda e, j=j: e.tensor_tensor(out=oc[:, 2 * hc + j, 0:W], in0=ps[:, pvb, 0:W], in1=rdc[pi][:, 0:W], op=ALU.mult),
                                  reads=[B(pvb), ('rdc', pi)], writes=[('oc', 2 * hc + j)])

                    ca_x(0)
                    for hc in range(4):
                        if hc + 1 < 4:
                            ca_x(hc + 1)
                        ca_y(hc)
                    pend2 = []
                    for tt in range(NTT):
                        for nb in range(2):
                            for kc in range(KC):
                                mm(ps[:, nb, :], oc[:, kc, tt * 128:(tt + 1) * 128], wco[:, kc, nb * 512:(nb + 1) * 512], kc == 0, kc == KC - 1,
                                   reads=['wco'] + [('oc', kc)], writes=[B(nb)])
                        while pend2:
                            pend2.pop(0)()
                        ya = ps[:, 0:2, :].rearrange("p a b -> p (a b)")
                        tk.op('dve', lambda e, tt=tt, ya=ya: e.scalar_tensor_tensor(out=yb[:], in0=x1[:, tt, :], scalar=ALPHA, in1=ya, op0=ALU.mult, op1=ALU.add),
                              reads=[('x1', tt), B(0), B(1)], writes=['yb'])
                        xi = tt % 2
                        layer_norm(lnb, yb[:], 'yb', lnv[:, 2, :], lnv[:, 3, :], x2[xi][:], ('x2', xi), 1)
                        tk.dma('sp', s_x2[t0 + tt * 128:t0 + (tt + 1) * 128, :], x2[xi][:], reads=[('x2', xi)])
                        tk.op('act', lambda e, xi=xi: e.activation(out=xbf[xi][:], in_=x2[xi][:], func=AF.Copy), reads=[('x2', xi)], writes=[('xbf', xi)])

                        def tr2(tt=tt, xi=xi):
                            bkt = 4 + xi
                            pb = ps[:, bkt, :].bitcast(BF16)
                            for kc in range(KC):
                                tk.op('pe', lambda e, kc=kc: e.transpose(pb[:, kc * 128:(kc + 1) * 128], xbf[xi][:, kc * 128:(kc + 1) * 128], IDB),
                                      reads=[('xbf', xi), 'cstb'], writes=[B(bkt)])
                            tk.op('act', lambda e: e.activation(out=x2ts[xi][:], in_=pb.rearrange("p (k t) -> p k t", k=KC), func=AF.Copy),
                                  reads=[B(bkt)], writes=[('x2ts', xi)])
                            tk.dma('sp', x2tv[:, :, t0 + tt * 128:t0 + (tt + 1) * 128], x2ts[xi][:], reads=[('x2ts', xi)])
                        pend2.append(tr2)
                    while pend2:
                        pend2.pop(0)()
                tk.barrier()

            with ExitStack() as st:
                wdn = sb(st, "wdn", [128, FC, D], BF16)
                wup = [sb(st, f"wup{i}", [128, KC, 256], BF16) for i in range(3)]
                xh = [sb(st, f"xh{i}", [128, KC, 514], BF16) for i in range(2)]
                ab = [sb(st, f"ab{i}", [128, 512], F32) for i in range(2)]
                ge = [sb(st, f"ge{i}", [128, 512], F32) for i in range(2)]
                hl = [sb(st, f"hl{i}", [128, 2], F32) for i in range(2)]
                vb = [sb(st, f"vb{i}", [128, 512], F32) for i in range(2)]
                ht = sb(st, "ht", [128, FC, 512], BF16)
                lnv3 = sb(st, "lnv3", [128, 2, D], F32)
                x2r = [sb(st, f"x2r{i}", [128, D], F32) for i in range(2)]
                y3 = [sb(st, f"y3_{i}", [128, D], F32) for i in range(2)]
                yo = [sb(st, f"yo_{i}", [128, D], F32) for i in range(2)]
                lnb = (sb(st, "lnst3", [128, 12], F32), sb(st, "lnmv3", [128, 2], F32), sb(st, "lnrs3", [128, 2], F32))
                cw = sb(st, "cws", [128, FC, 4], F32)
                tk.dma('sp', cw[:], cw_d[s], writes=['cw'])
                tk.dma('sp', lnv3[:], lnv_d[:, 4:6, :], writes=['lnv'])
                x2tv = s_x2t.rearrange("(kc p) t -> p kc t", p=128)
                wc = 0
                def load_xh(pc):
                    t0 = pc * 512
                    xi = pc % 2
                    lo = 1 if pc == 0 else 0
                    hi = 513 if (pc == QPC - 1 and not HALO) else 514
                    if pc == 0:
                        tk.op('pool', lambda e: e.memset(xh[xi][:, :, 0:1], 0.0), writes=[('xh', xi)])
                    if pc == QPC - 1 and not HALO:
                        tk.op('pool', lambda e: e.memset(xh[xi][:, :, 513:514], 0.0), writes=[('xh', xi)])
                    for kc in range(KC):
                        tk.dma('sp', xh[xi][:, kc, lo:hi], x2tv[:, kc, t0 - 1 + lo:t0 - 1 + hi], writes=[('xh', xi)])

                load_xh(0)
                for pc in range(QPC):
                    t0 = pc * 512
                    xi = pc % 2
                    if pc + 1 < QPC:
                        load_xh(pc + 1)
                    for fc in range(FC):
                        if pc == 0:
                            tk.dma('pool', wdn[:, fc, :], wdn_d[:, fc, :], writes=[('wdn', fc)])
                        wi = wc % 3
                        if wc == 0:
                            for pf in range(2):
                                tk.dma('act', wup[pf][:].rearrange("p k c -> p (k c)"), s_wup[pf], writes=[('wup', pf)])
                        nxt = wc + 2
                        if nxt < QPC * FC:
                            tk.dma('act', wup[nxt % 3][:].rearrange("p k c -> p (k c)"), s_wup[nxt % FC], writes=[('wup', nxt % 3)])
                        wc += 1
                        bg = (fc % 2) * 3
                        bv = bg + 1
                        bh = bg + 2
                        for kc in range(KC):
                            mm(ps[:, bg, :], wup[wi][:, kc, 0:128], xh[xi][:, kc, 1:513], kc == 0, kc == KC - 1,
                               reads=[('wup', wi), ('xh', xi)], writes=[B(bg)])
                        for kc in range(KC):
                            mm(ps[:, bh, 0:2], wup[wi][:, kc, 0:128], xh[xi][:, kc, 0:514:513], kc == 0, kc == KC - 1,
                               reads=[('wup', wi), ('xh', xi)], writes=[B(bh)])
                        for kc in range(KC):
                            mm(ps[:, bv, :], wup[wi][:, kc, 128:256], xh[xi][:, kc, 1:513], kc == 0, kc == KC - 1,
                               reads=[('wup', wi), ('xh', xi)], writes=[B(bv)])
                        ai = fc % 2
                        tk.op('dve', lambda e, ai=ai, bg=bg, fc=fc: e.tensor_scalar(out=ab[ai][:], in0=ps[:, bg, :], scalar1=cw[:, fc, 1:2], scalar2=cw[:, fc, 3:4], op0=ALU.mult, op1=ALU.add),
                              reads=[B(bg), 'cw'], writes=[('ab', ai)])
                        tk.op('dve', lambda e, ai=ai, bg=bg, fc=fc: e.scalar_tensor_tensor(out=ab[ai][:, 1:512], in0=ps[:, bg, 0:511], scalar=cw[:, fc, 0:1], in1=ab[ai][:, 1:512], op0=ALU.mult, op1=ALU.add),
                              reads=[B(bg), 'cw', ('ab', ai)], writes=[('ab', ai)])
                        tk.op('dve', lambda e, ai=ai, bg=bg, fc=fc: e.scalar_tensor_tensor(out=ab[ai][:, 0:511], in0=ps[:, bg, 1:512], scalar=cw[:, fc, 2:3], in1=ab[ai][:, 0:511], op0=ALU.mult, op1=ALU.add),
                              reads=[B(bg), 'cw', ('ab', ai)], writes=[('ab', ai)])
                        tk.op('act', lambda e, ai=ai, bh=bh: e.activation(out=hl[ai][:], in_=ps[:, bh, 0:2], func=AF.Copy),
                              reads=[B(bh)], writes=[('hl', ai)])
                        tk.op('act', lambda e, ai=ai, bv=bv: e.activation(out=vb[ai][:], in_=ps[:, bv, :], func=AF.Copy),
                              reads=[B(bv)], writes=[('vb', ai)])
                        tk.op('pool', lambda e, ai=ai, fc=fc: e.tensor_tensor(out=hl[ai][:], in0=hl[ai][:], in1=cw[:, fc, 0:3:2], op=ALU.mult),
                              reads=[('hl', ai), 'cw'], writes=[('hl', ai)])
                        tk.op('pool', lambda e, ai=ai: e.tensor_tensor(out=ab[ai][:, 0:512:511], in0=ab[ai][:, 0:512:511], in1=hl[ai][:], op=ALU.add),
                              reads=[('hl', ai), ('ab', ai)], writes=[('ab', ai)])
                        tk.op('act', lambda e, ai=ai: e.activation(out=ge[ai][:], in_=ab[ai][:], func=AF.Gelu), reads=[('ab', ai)], writes=[('ge', ai)])
                        tk.op('pool', lambda e, ai=ai, fc=fc: e.tensor_tensor(out=ht[:, fc, :], in0=ge[ai][:], in1=vb[ai][:], op=ALU.mult),
                              reads=[('vb', ai), ('ge', ai)], writes=[('ht', fc)])
                    for tt in range(4):
                        ri = tt % 2
                        tk.dma('sp', x2r[ri][:], s_x2[t0 + tt * 128:t0 + (tt + 1) * 128, :], writes=[('x2r', ri)])
                        for nb in range(2):
                            for fc in range(FC):
                                mm(ps[:, 6 + nb, :], ht[:, fc, tt * 128:(tt + 1) * 128], wdn[:, fc, nb * 512:(nb + 1) * 512], fc == 0, fc == FC - 1,
                                   reads=[('wdn', fc), ('ht', fc)], writes=[B(6 + nb)])
                        ya = ps[:, 6:8, :].rearrange("p a b -> p (a b)")
                        tk.op('dve', lambda e, ri=ri, ya=ya: e.scalar_tensor_tensor(out=y3[ri][:], in0=x2r[ri][:], scalar=ALPHA, in1=ya, op0=ALU.mult, op1=ALU.add),
                              reads=[('x2r', ri), B(6), B(7)], writes=[('y3', ri)])
                        layer_norm(lnb, y3[ri][:], ('y3', ri), lnv3[:, 0, :], lnv3[:, 1, :], yo[ri][:], ('yo', ri), 2)
                        tk.dma('sp', y_d[s, t0 + tt * 128:t0 + (tt + 1) * 128, :], yo[ri][:], reads=[('yo', ri)])
                tk.barrier()
        tk.barrier()
    return nc


def _bf16_round(a):
    return a


def host_constants(T):
    pos = np.arange(T, dtype=np.float64)
    p = np.arange(128)
    d = p % 64
    inv = THETA ** (-(np.arange(0, 64, 2, dtype=np.float64)) / 64.0)
    f = d % 32
    ang = pos[None, :] * inv[f][:, None]
    ca = np.cos(ang)
    sa = np.sin(ang) * np.where(d < 32, -1.0, 1.0)[:, None]
    inv16 = THETA ** (-(np.arange(0, 32, 2, dtype=np.float64)) / 32.0)
    dd = d % 32
    fb = dd % 16
    row = np.floor(pos / 64.0)
    col = pos % 64.0
    pp = np.where((d < 32)[:, None], row[None, :], col[None, :])
    angb = pp * inv16[fb][:, None]
    cb = np.cos(angb)
    sbb = np.sin(angb) * np.where(dd < 16, -1.0, 1.0)[:, None]
    rope = np.stack([ca, sa, cb, sbb], axis=1).astype(np.float32)
    i = np.arange(128)[:, None, None]
    j = np.arange(NMASK)[None, :, None]
    q = np.arange(512)[None, None, :]
    dl = -1024 + 128 * j + i - q
    ad = np.abs(dl)
    mult = (ad <= 64).astype(np.float32) + ((dl % 4 == 0) & (ad <= 256)) + ((dl % 16 == 0) & (ad <= 1024))
    mask = mult.astype(np.float32)
    cst = np.zeros((128, 3, 128), np.float32)
    cst[:, 0, :] = np.eye(128)
    cst[0:64, 1, 0:64] = 1.0
    cst[64:128, 1, 64:128] = 1.0
    cst[:, 2, :] = 1.0
    sel = np.zeros((16, KC, 128), np.float32)
    for h in range(16):
        sel[h, h // 2, (h % 2) * 64:(h % 2) * 64 + 64] = 1.0
    return rope, mask, cst, sel.reshape(16, KC * 128)


PERM_A = np.concatenate([np.arange(32, 64), np.arange(0, 32)])
PERM_B = np.concatenate([np.arange(16, 32), np.arange(0, 16), np.arange(48, 64), np.arange(32, 48)])


def _pk(w):
    k, n = w.shape
    return np.ascontiguousarray(w.reshape(k // 128, 128, n).transpose(1, 0, 2))


def _wup_blocked(w):
    g = w[:, :DFF].reshape(KC, 128, FC, 128)
    v = w[:, DFF:].reshape(KC, 128, FC, 128)
    gv = np.concatenate([g, v], axis=3)
    return np.ascontiguousarray(gv.transpose(2, 1, 0, 3).reshape(FC, 128, KC * 256))


def host_weights(w_in, q_norm_g, k_norm_g, out_norm_a, out_norm_b, w_o, ln1_g, ln1_b, wc_q, wc_k, wc_v, wc_o,
                 ln2_g, ln2_b, w_up, conv_w, conv_b, w_down, ln3_g, ln3_b):
    w = w_in[0]
    qa, ka, va, qb, kb, vb = w[:, 0:512], w[:, 512:1024], w[:, 1024:1536], w[:, 1536:2048], w[:, 2048:2176], w[:, 2176:2304]

    def swp(blk, perm):
        n = blk.shape[1] // 64
        return blk.reshape(D, n, 64)[:, :, perm].reshape(D, n * 64)

    kb0, kb1 = kb[:, 0:64], kb[:, 64:128]
    kb0s, kb1s = kb0[:, PERM_B], kb1[:, PERM_B]
    cat = np.concatenate([qa, swp(qa, PERM_A), ka, swp(ka, PERM_A), va, qb, swp(qb, PERM_B),
                          kb0, kb0, kb1, kb1, kb0s, kb0s, kb1s, kb1s, vb], axis=1)
    assert cat.shape[1] == NBLK * 128
    lnv = np.stack([ln1_g[0], ln1_b[0], ln2_g[0], ln2_b[0], ln3_g[0], ln3_b[0]], axis=0)
    lnv = np.ascontiguousarray(np.broadcast_to(lnv[None], (128, 6, D))).astype(np.float32)
    cwv = np.concatenate([conv_w[0], conv_b], axis=0)
    cwv = np.ascontiguousarray(cwv.reshape(4, FC, 128).transpose(2, 1, 0)).astype(np.float32)
    gov = np.concatenate([out_norm_a[0], out_norm_b[0]])
    gov = np.ascontiguousarray(gov.reshape(KC, 128).T).astype(np.float32)
    dd = np.arange(128) % 64
    gqv = np.stack([q_norm_g[0][dd], q_norm_g[0][PERM_B[dd]], k_norm_g[0][dd], k_norm_g[0][PERM_B[dd]]], axis=1).astype(np.float32)
    return dict(wcat=_pk(cat), wo=_pk(w_o[0]), wcq=_pk(wc_q[0]), wck=_pk(wc_k[0]), wcv=_pk(wc_v[0]), wco=_pk(wc_o[0]),
                wup=_wup_blocked(w_up[0]), wdn=_pk(w_down[0]), lnv=lnv, cw=cwv, go=gov, gq=gqv)


def run_sequences(xs, mems, wts, T, NS, n_cores):
    rope, mask, cst, sel = host_constants(T)
    QT = T // 2
    QX = QT + 128
    nseq = len(xs)
    tasks = [(q, h) for q in range(nseq) for h in (0, 1)]
    assert len(tasks) <= NS * n_cores
    wts = dict(wts)
    cw_nat = wts.pop('cw')
    cw_mir = np.ascontiguousarray(cw_nat[:, :, [2, 1, 0, 3]])
    nat = np.arange(T)
    mir = nat[::-1].copy()
    in_maps = []
    for c in range(n_cores):
        xT = np.zeros((NS, D, T), np.float32)
        xx = np.zeros((NS, QX, D), np.float32)
        mT = np.zeros((NS, D, NMEM), np.float32)
        rp = np.zeros((NS, 128, 4, T), np.float32)
        cw = np.zeros((NS, 128, FC, 4), np.float32)
        for s in range(NS):
            idx = s * n_cores + c
            if idx >= len(tasks):
                idx = 0
            q, h = tasks[idx]
            order = nat if h == 0 else mir
            xb = xs[q][order]
            xx[s] = xb[:QX]
            xT[s] = xb.T
            mT[s] = mems[q].T
            rp[s] = rope[:, :, order]
            cw[s] = cw_nat if h == 0 else cw_mir
        m = dict(xT=xT, x=xx, memT=mT, rope=rp, cw=cw, mask=mask, cst=cst, sel=sel)
        m.update(wts)
        in_maps.append(m)
    nc = build_program(T, NS, QT)
    res = run_bass_kernel_spmd(nc, in_maps, core_ids=list(range(n_cores)))
    outs = [np.empty((T, D), np.float32) for _ in range(nseq)]
    for idx, (q, h) in enumerate(tasks):
        s, c = divmod(idx, n_cores)
        yv = np.asarray(res.results[c]["y"][s], dtype=np.float32)
        order = nat if h == 0 else mir
        outs[q][order[:QT]] = yv
    return outs


def kernel(x_prompt, x_sample, mem_prompt, mem_sample, w_in, q_norm_g, k_norm_g, out_norm_a, out_norm_b,
           w_o, ln1_g, ln1_b, wc_q, wc_k, wc_v, wc_o, ln2_g, ln2_b, w_up, conv_w, conv_b, w_down, ln3_g, ln3_b):
    f = lambda a: np.asarray(a, dtype=np.float32)
    wts = host_weights(f(w_in), f(q_norm_g), f(k_norm_g), f(out_norm_a), f(out_norm_b), f(w_o), f(ln1_g), f(ln1_b),
                       f(wc_q), f(wc_k), f(wc_v), f(wc_o), f(ln2_g), f(ln2_b), f(w_up), f(conv_w), f(conv_b),
                       f(w_down), f(ln3_g), f(ln3_b))
    xp, xs_, mp, ms = f(x_prompt), f(x_sample), f(mem_prompt), f(mem_sample)
    seqs = [xp[i] for i in range(xp.shape[0])] + [xs_[i] for i in range(xs_.shape[0])]
    mems = [mp[i] for i in range(mp.shape[0])] + [ms[i] for i in range(ms.shape[0])]
    T = seqs[0].shape[0]
    outs = run_sequences(seqs, mems, wts, T, NS=3, n_cores=8)
    nb = xp.shape[0]
    return (np.stack(outs[:nb], axis=0), np.stack(outs[nb:], axis=0))
```
